# Optimizing a Trainium2 kernel written in Bass

```python
import math
import jax, jax.numpy as jnp
from jax import lax
import numpy as np

D_MODEL = 4096
BATCH = 8
SEQ = 2048
DEPTH = 2
DEC_BATCH = 4
DEC_SEQ = 4096
PAST_LEN = 128

MIX_WIDTH = D_MODEL
RET_HEADS = 8
RET_QK_DIM = MIX_WIDTH // (4 * RET_HEADS)
RET_V_DIM = 2 * RET_QK_DIM
RET_QK_WIDTH = RET_HEADS * RET_QK_DIM
RET_WIDTH = RET_HEADS * RET_V_DIM
ATT_HEADS = 16
ATT_HEAD_DIM = MIX_WIDTH // (2 * ATT_HEADS)
ATT_WIDTH = ATT_HEADS * ATT_HEAD_DIM
IN_WIDTH = 2 * RET_QK_WIDTH + 2 * RET_WIDTH + 3 * ATT_WIDTH
D_FF = 4 * D_MODEL
RET_CHUNK = 128
ROPE_BASE = 10000.0
DILATED_BRANCHES = ((128, 1), (512, 4), (2048, 16))
REL_BUCKETS = 32
REL_MAX_DISTANCE = 1024
NORM_EPS = 1e-6
NEG_INF = -1e30

kernel_name = 'hymba_retention_dilated_attn_encoder'


def _rmsnorm(x, gain):
    xf = x.astype(jnp.float32)
    y = xf * lax.rsqrt(jnp.mean(xf * xf, axis=-1, keepdims=True) + NORM_EPS)
    return (y * gain.astype(jnp.float32)).astype(x.dtype)


def _rotary(x):
    S, d = x.shape[1], x.shape[-1]
    half = d // 2
    inv = ROPE_BASE ** (-jnp.arange(half, dtype=jnp.float32) / half)
    ang = jnp.arange(S, dtype=jnp.float32)[:, None] * inv[None, :]
    cos = jnp.cos(ang)[None, :, None, :]
    sin = jnp.sin(ang)[None, :, None, :]
    xf = x.astype(jnp.float32)
    x1, x2 = xf[..., :half], xf[..., half:]
    return jnp.concatenate([x1 * cos - x2 * sin, x1 * sin + x2 * cos], axis=-1).astype(x.dtype)


def _retention_dir(q, k, v, log_decay, include_diag):
    B, H, S, dk = q.shape
    dv = v.shape[-1]
    C = RET_CHUNK
    nc = S // C
    dt = q.dtype
    qc = q.reshape(B, H, nc, C, dk)
    kc = k.reshape(B, H, nc, C, dk)
    vc = v.reshape(B, H, nc, C, dv)
    idx = jnp.arange(C, dtype=jnp.float32)
    diff = idx[:, None] - idx[None, :]
    allowed = diff >= 0 if include_diag else diff > 0
    ld = log_decay[:, None, None]
    inner_mask = jnp.where(allowed[None], jnp.exp(ld * jnp.maximum(diff, 0.0)[None]), 0.0).astype(dt)
    scores = jnp.einsum('bhnid,bhnjd->bhnij', qc, kc) * inner_mask[None, :, None]
    inner = jnp.einsum('bhnij,bhnje->bhnie', scores, vc)
    k_decay = jnp.exp(log_decay[:, None] * (C - 1 - idx)[None, :]).astype(dt)
    q_decay = jnp.exp(log_decay[:, None] * (idx + 1)[None, :]).astype(dt)
    chunk_decay = jnp.exp(log_decay * C).astype(dt)[None, :, None, None]
    delta = jnp.einsum('bhnjd,bhnje->nbhde', kc * k_decay[None, :, None, :, None], vc)

    def step(state, d):
        return state * chunk_decay + d, state

    _, prev = lax.scan(step, jnp.zeros((B, H, dk, dv), dt), delta)
    cross = jnp.einsum('bhnid,nbhde->bhnie', qc * q_decay[None, :, None, :, None], prev)
    return (inner + cross).reshape(B, H, S, dv)


def _retention_group(rq, rk, rv, rg, ret_log_decay, ret_norm_gain):
    B, S, _ = rq.shape
    q = _rotary(rq.reshape(B, S, RET_HEADS, RET_QK_DIM))
    k = _rotary(rk.reshape(B, S, RET_HEADS, RET_QK_DIM)) * (RET_QK_DIM ** -0.5)
    v = rv.reshape(B, S, RET_HEADS, RET_V_DIM)
    q, k, v = (t.transpose(0, 2, 1, 3) for t in (q, k, v))
    log_decay = -jnp.exp(ret_log_decay.astype(jnp.float32))
    y_fwd = _retention_dir(q, k, v, log_decay[0], True)
    y_bwd = _retention_dir(q[:, :, ::-1], k[:, :, ::-1], v[:, :, ::-1], log_decay[1], False)[:, :, ::-1]
    y = (y_fwd + y_bwd).transpose(0, 2, 1, 3)
    y = _rmsnorm(y, ret_norm_gain)
    y = jax.nn.silu(rg.reshape(B, S, RET_HEADS, RET_V_DIM)) * y
    return y.reshape(B, S, RET_WIDTH)


def _rel_bucket(rel):
    nbk = REL_BUCKETS // 2
    max_exact = nbk // 2
    base = jnp.where(rel > 0, nbk, 0)
    n = jnp.abs(rel)
    nf = jnp.maximum(n, 1).astype(jnp.float32)
    large = max_exact + (jnp.log(nf / max_exact) / math.log(REL_MAX_DISTANCE / max_exact)
                         * (nbk - max_exact)).astype(jnp.int32)
    large = jnp.minimum(large, nbk - 1)
    return base + jnp.where(n < max_exact, n, large)


def _key_windows(x, L, blk, nb):
    B, G, _, H, d = x.shape
    Lp = nb * blk
    xp = jnp.pad(x, ((0, 0), (0, 0), (blk, Lp - L + blk), (0, 0), (0, 0)))
    xb = xp.reshape(B, G, nb + 2, blk, H, d)
    return jnp.concatenate([xb[:, :, :-2], xb[:, :, 1:-1], xb[:, :, 2:]], axis=3)


def _dilated_branch(q, k, v, rel_bias_table, window, dilation):
    B, S, H, dh = q.shape
    half = window // (2 * dilation)
    blk = half
    L = S // dilation
    nb = -(-L // blk)
    Lp = nb * blk

    def to_res(t):
        return t.reshape(B, L, dilation, H, dh).transpose(0, 2, 1, 3, 4)

    qr, kr, vr = to_res(q), to_res(k), to_res(v)
    qb = jnp.pad(qr, ((0, 0), (0, 0), (0, Lp - L), (0, 0), (0, 0))).reshape(B, dilation, nb, blk, H, dh)
    kw = _key_windows(kr, L, blk, nb)
    vw = _key_windows(vr, L, blk, nb)
    s = jnp.einsum('bgnqhd,bgnkhd->bgnhqk', qb, kw, preferred_element_type=jnp.float32)
    delta = jnp.arange(3 * blk)[None, :] - blk - jnp.arange(blk)[:, None]
    bias = rel_bias_table[_rel_bucket(delta * dilation)]
    s = s + jnp.transpose(bias, (2, 0, 1)).astype(jnp.float32)
    key_pos = jnp.arange(nb)[:, None] * blk + jnp.arange(3 * blk)[None, :] - blk
    valid = (jnp.abs(delta) <= half)[None] & ((key_pos >= 0) & (key_pos < L))[:, None, :]
    s = jnp.where(valid[None, None, :, None], s, NEG_INF)
    m = jnp.max(s, axis=-1, keepdims=True)
    p = jnp.exp(s - m)
    den = jnp.sum(p, axis=-1)
    o = jnp.einsum('bgnhqk,bgnkhd->bgnqhd', p.astype(v.dtype), vw, preferred_element_type=jnp.float32)
    o = o / jnp.transpose(den, (0, 1, 2, 4, 3))[..., None]
    lse = jnp.transpose(m[..., 0] + jnp.log(den), (0, 1, 2, 4, 3))
    o = o.reshape(B, dilation, Lp, H, dh)[:, :, :L].transpose(0, 2, 1, 3, 4).reshape(B, S, H, dh)
    lse = lse.reshape(B, dilation, Lp, H)[:, :, :L].transpose(0, 2, 1, 3).reshape(B, S, H)
    return o, lse


def _dilated_attention_group(aq, ak, av, rel_bias_table):
    B, S, _ = aq.shape
    q = aq.reshape(B, S, ATT_HEADS, ATT_HEAD_DIM) * (ATT_HEAD_DIM ** -0.5)
    k = ak.reshape(B, S, ATT_HEADS, ATT_HEAD_DIM)
    v = av.reshape(B, S, ATT_HEADS, ATT_HEAD_DIM)
    outs, lses = [], []
    for window, dilation in DILATED_BRANCHES:
        o, l = _dilated_branch(q, k, v, rel_bias_table, window, dilation)
        outs.append(o)
        lses.append(l)
    w = jax.nn.softmax(jnp.stack(lses, axis=0), axis=0)
    o = jnp.sum(w[..., None] * jnp.stack(outs, axis=0), axis=0)
    return o.astype(aq.dtype).reshape(B, S, ATT_WIDTH)


def _mixer(h, rel_bias_table, w_in, ret_log_decay, ret_norm_gain, w_out):
    proj = h @ w_in
    c1 = RET_QK_WIDTH
    c2 = c1 + RET_QK_WIDTH
    c3 = c2 + RET_WIDTH
    c4 = c3 + RET_WIDTH
    c5 = c4 + ATT_WIDTH
    c6 = c5 + ATT_WIDTH
    rq, rk, rv, rg, aq, ak, av = jnp.split(proj, (c1, c2, c3, c4, c5, c6), axis=-1)
    y_ret = _retention_group(rq, rk, rv, rg, ret_log_decay, ret_norm_gain)
    y_att = _dilated_attention_group(aq, ak, av, rel_bias_table)
    return jnp.concatenate([y_ret, y_att], axis=-1) @ w_out


def _trunk(x, rel_bias_table, w_in, ret_log_decay, ret_norm_gain, w_out, w_up, w_down,
           norm_mix_pre, norm_mix_post, norm_mlp_pre, norm_mlp_post):
    for l in range(DEPTH):
        h = _rmsnorm(x, norm_mix_pre[l])
        x = x + _rmsnorm(_mixer(h, rel_bias_table, w_in[l], ret_log_decay[l], ret_norm_gain[l], w_out[l]),
                         norm_mix_post[l])
        h = _rmsnorm(x, norm_mlp_pre[l])
        u = jnp.maximum(h @ w_up[l], 0)
        x = x + _rmsnorm((u * u) @ w_down[l], norm_mlp_post[l])
    return x


def setup_inputs(seed: int = 0) -> dict:
    key = jax.random.key(seed)
    ks = jax.random.split(key, 13)
    f32 = jnp.float32
    base_decay = np.log(-np.log(1.0 - 2.0 ** (-5.0 - np.arange(RET_HEADS)))).astype(np.float32)
    return {
        'x_prompt': jax.random.normal(ks[0], (BATCH, SEQ, D_MODEL), f32),
        'x_sample': jax.random.normal(ks[1], (DEC_BATCH, DEC_SEQ, D_MODEL), f32),
        'rel_bias_table': 0.5 * jax.random.normal(ks[2], (REL_BUCKETS, ATT_HEADS), f32),
        'w_in': jax.random.normal(ks[3], (DEPTH, D_MODEL, IN_WIDTH), f32) * D_MODEL ** -0.5,
        'ret_log_decay': jnp.asarray(base_decay)[None, None, :]
                         + 0.1 * jax.random.normal(ks[4], (DEPTH, 2, RET_HEADS), f32),
        'ret_norm_gain': 1.0 + 0.02 * jax.random.normal(ks[5], (DEPTH, RET_HEADS, RET_V_DIM), f32),
        'w_out': jax.random.normal(ks[6], (DEPTH, MIX_WIDTH, D_MODEL), f32) * MIX_WIDTH ** -0.5,
        'w_up': jax.random.normal(ks[7], (DEPTH, D_MODEL, D_FF), f32) * D_MODEL ** -0.5,
        'w_down': jax.random.normal(ks[8], (DEPTH, D_FF, D_MODEL), f32) * D_FF ** -0.5,
        'norm_mix_pre': 1.0 + 0.02 * jax.random.normal(ks[9], (DEPTH, D_MODEL), f32),
        'norm_mix_post': 1.0 + 0.02 * jax.random.normal(ks[10], (DEPTH, D_MODEL), f32),
        'norm_mlp_pre': 1.0 + 0.02 * jax.random.normal(ks[11], (DEPTH, D_MODEL), f32),
        'norm_mlp_post': 1.0 + 0.02 * jax.random.normal(ks[12], (DEPTH, D_MODEL), f32),
    }


def reference(x_prompt, x_sample, rel_bias_table, w_in, ret_log_decay, ret_norm_gain, w_out, w_up, w_down,
              norm_mix_pre, norm_mix_post, norm_mlp_pre, norm_mlp_post):
    y_prompt = _trunk(x_prompt, rel_bias_table, w_in, ret_log_decay, ret_norm_gain, w_out, w_up, w_down,
                      norm_mix_pre, norm_mix_post, norm_mlp_pre, norm_mlp_post)
    y_sample = _trunk(x_sample, rel_bias_table, w_in, ret_log_decay, ret_norm_gain, w_out, w_up, w_down,
                      norm_mix_pre, norm_mix_post, norm_mlp_pre, norm_mlp_post)
    return (y_prompt, y_sample)
```

```python
import os
from contextlib import ExitStack
import numpy as np
import concourse.bass as bass
import concourse.mybir as mybir
from concourse.ap import AP
from concourse.bass_utils import run_bass_kernel_spmd

F32 = mybir.dt.float32
BF16 = mybir.dt.bfloat16
ALU = mybir.AluOpType
AF = mybir.ActivationFunctionType

D = 4096
NTOK = 4096
T = 512
NT = NTOK // T
KC = 32
DEPTH = 2
INW = 12288
DFF = 16384
EPS = 1e-6
NEG = -30000.0
C = 128
NCH = NTOK // C

CO_DPOS = 0
CO_DNEG = 128
CO_IP1 = 256
CO_CMI = 384
CO_IDN = 512
CO_SWP = 640
CO_CM1J = 768
CO_JCOL = 769
CO_C128 = 770
CO_NEGC = 771
CO_KEEPF = 772
CO_KEEPB = 804
CO_N = 836


class Res:
    __slots__ = ("w", "re", "rd", "excl")

    def __init__(self, excl=False):
        self.w = None
        self.re = {}
        self.rd = []
        self.excl = excl

    def add_read(self, tok):
        if tok[0] == "e":
            if self.re.get(tok[1], 0) < tok[2]:
                self.re[tok[1]] = tok[2]
        else:
            self.rd.append(tok)

    def set_write(self, tok):
        self.w = tok
        self.re = {}
        self.rd = []


class Tk:
    def __init__(self, nc, nslots=22):
        self.nc = nc
        self.E = {}
        for name, eng in (("pe", nc.tensor), ("dve", nc.vector), ("act", nc.scalar),
                          ("pool", nc.gpsimd), ("sp", nc.sync)):
            self.E[name] = dict(eng=eng, sem=nc.alloc_semaphore("s_" + name), count=0, seen={}, seen_d={})
        self.dq = {}
        for q in ("sp", "pool"):
            self.dq[q] = dict(slots=[dict(sem=nc.alloc_semaphore(f"d_{q}{i}"), val=0, id=(q, i))
                                     for i in range(nslots)], next=0)

    def _wait(self, en, dep):
        e = self.E[en]
        if dep[0] == "e":
            _, name, c = dep
            if name == en and en in ("pe", "sp"):
                return
            if e["seen"].get(name, 0) >= c:
                return
            e["eng"].wait_ge(self.E[name]["sem"], c)
            e["seen"][name] = c
        else:
            _, slot, v = dep
            if e["seen_d"].get(slot["id"], 0) >= v:
                return
            e["eng"].wait_ge(slot["sem"], v)
            e["seen_d"][slot["id"]] = v

    def _waitall(self, en, reads, writes):
        for r in reads:
            if r.w is not None:
                self._wait(en, r.w)
            if r.excl:
                for n, c in list(r.re.items()):
                    if n != en:
                        self._wait(en, ("e", n, c))
        for w in writes:
            if w.w is not None:
                self._wait(en, w.w)
            for n, c in list(w.re.items()):
                self._wait(en, ("e", n, c))
            for d in w.rd:
                self._wait(en, d)

    def op(self, en, fn, reads=(), writes=(), inc=True):
        e = self.E[en]
        self._waitall(en, reads, writes)
        ins = fn(e["eng"])
        tok = ("e", en, e["count"] + 1)
        if inc:
            ins.then_inc(e["sem"], 1)
            e["count"] += 1
        for r in reads:
            r.add_read(tok)
        for w in writes:
            w.set_write(tok)
        return ins

    def dma(self, q, out, in_, reads=(), writes=(), **kw):
        e = self.E[q]
        dq = self.dq[q]
        slot = dq["slots"][dq["next"] % len(dq["slots"])]
        dq["next"] += 1
        self._waitall(q, reads, writes)
        if slot["val"] > 0:
            self._wait(q, ("d", slot, slot["val"]))
        ins = e["eng"].dma_start(out=out, in_=in_, **kw)
        slot["val"] += 16
        ins.then_inc(slot["sem"], 16)
        tok = ("d", slot, slot["val"])
        for r in reads:
            r.add_read(tok)
        for w in writes:
            w.set_write(tok)
        return ins

    def barrier(self):
        for en, e in self.E.items():
            for on, oe in self.E.items():
                if on != en and oe["count"] > 0:
                    self._wait(en, ("e", on, oe["count"]))
            for q in self.dq.values():
                for slot in q["slots"]:
                    if slot["val"] > 0:
                        self._wait(en, ("d", slot, slot["val"]))


def _col_kind(g):
    c0 = g * 256
    if c0 < 1024:
        return "rq"
    if c0 < 2048:
        return "rk"
    if c0 < 4096:
        return "rv"
    if c0 < 6144:
        return "rg"
    if c0 < 8192:
        return "aq"
    if c0 < 10240:
        return "ak"
    return "av"


def build(dbg=None):
    dbg = dbg or {}
    phases = dbg.get("phases", None)
    dump = dbg.get("dump", False)
    nlayers = dbg.get("layers", DEPTH)

    def want(p):
        return phases is None or p in phases

    nc = bass.Bass("TRN2", target_bir_lowering=False)
    skind = "ExternalOutput" if dump else "Internal"

    def din(name, shape, dt=F32):
        return nc.dram_tensor(name, list(shape), dt, kind="ExternalInput").ap()

    x_in = din("x", [NTOK, D])
    smallw = dbg.get("smallw", ())
    w_in = din("w_in", [DEPTH, D, INW] if "w_in" not in smallw else [1, 1, 1])
    w_out = din("w_out", [DEPTH, D, D] if "w_out" not in smallw else [1, 1, 1])
    w_up = din("w_up", [DEPTH, D, DFF] if "w_up" not in smallw else [1, 1, 1])
    w_down = din("w_down", [DEPTH, DFF, D] if "w_down" not in smallw else [1, 1, 1])
    relb = din("rel_bias_table", [32, 16])
    rld = din("ret_log_decay", [DEPTH, 2, 8])
    rng = din("ret_norm_gain", [DEPTH, 8, 256])
    g_mix_pre = din("norm_mix_pre", [DEPTH, D])
    g_mix_post = din("norm_mix_post", [DEPTH, D])
    g_mlp_pre = din("norm_mlp_pre", [DEPTH, D])
    g_mlp_post = din("norm_mlp_post", [DEPTH, D])
    rot = din("rot", [4, 128, NTOK])
    consts = din("consts", [128, CO_N])
    onehot = din("onehot", [3, 33, 384])
    y_out = nc.dram_tensor("y", [NTOK, D], F32, kind="ExternalOutput").ap()

    XT_t = nc.dram_tensor("XT", [KC, 128, NTOK], F32, kind=skind)
    X1T_t = nc.dram_tensor("X1T", [KC, 128, NTOK], F32, kind=skind)
    FT_t = nc.dram_tensor("FT", [64, 128, NTOK], BF16, kind=skind)
    VT_t = nc.dram_tensor("VT", [NTOK, 4096], BF16, kind=skind)
    YT_t = nc.dram_tensor("YT", [KC, 128, NTOK], BF16, kind=skind)
    U_t = nc.dram_tensor("U", [3, 16, 129, 384], F32, kind=skind)
    XT, X1T, FT, VT, YT, U = (t.ap() for t in (XT_t, X1T_t, FT_t, VT_t, YT_t, U_t))

    tk = Tk(nc)
    op = tk.op
    dma = tk.dma

    with ExitStack() as top:
        uid = [0]

        def sb(name, shape, dt, es=top):
            uid[0] += 1
            return es.enter_context(nc.sbuf_tensor(f"{name}_{uid[0]}", list(shape), dt))

        ps = top.enter_context(nc.psum_tensor("ps", [128, 8, 512], F32))
        psr = [Res(excl=True) for _ in range(8)]

        cst = sb("cst", [128, CO_N], F32)
        cst_r = Res()
        dma("sp", cst[:], consts[:, :], writes=[cst_r])
        idn_bf = sb("idn_bf", [128, 128], BF16)
        swp_bf = sb("swp_bf", [128, 128], BF16)
        ones_bf = sb("ones_bf", [128, 128], BF16)
        om4096 = sb("om4096", [128, 128], BF16)
        om256 = sb("om256", [128, 128], BF16)
        epsc = sb("epsc", [128, 1], F32)
        k_r = Res()
        op("dve", lambda e: e.tensor_copy(idn_bf[:], cst[:, CO_IDN:CO_IDN + 128]), reads=[cst_r], writes=[k_r])
        op("dve", lambda e: e.tensor_copy(swp_bf[:], cst[:, CO_SWP:CO_SWP + 128]), reads=[cst_r], writes=[k_r])
        op("dve", lambda e: e.memset(ones_bf[:], 1.0), writes=[k_r])
        op("dve", lambda e: e.memset(om4096[:], 1.0 / 4096.0), writes=[k_r])
        op("dve", lambda e: e.memset(om256[:], 1.0 / 256.0), writes=[k_r])
        op("dve", lambda e: e.memset(epsc[:], EPS), writes=[k_r])
        idn32 = cst[:, CO_IDN:CO_IDN + 128]
        gains = sb("gains", [128, 4 * DEPTH, KC], F32)
        rgn = sb("rgn", [128, DEPTH * 8 * 2], F32)
        gains_r = Res()
        with ExitStack() as es0:
            gst = sb("gst", [128, 3, 128], F32, es0)
            gst_r = Res()
            op("dve", lambda e: e.memset(gst[:], 0.0), writes=[gst_r])
            for gi, gsrc in enumerate((g_mix_pre, g_mix_post, g_mlp_pre, g_mlp_post)):
                for l in range(DEPTH):
                    v = gi * DEPTH + l
                    dma("sp", gst[(v % 4) * 32:(v % 4 + 1) * 32, v // 4, :], gsrc[l].rearrange("(k p) -> k p", p=128),
                        writes=[gst_r])
            dma("sp", gst[0:32, 2, :], rng.rearrange("l h (e p) -> (l h e) p", p=128), writes=[gst_r])
            for i3 in range(3):
                op("pe", lambda e: e.transpose(ps[:, 0, i3 * 128:(i3 + 1) * 128], gst[:, i3, :], cst[:, CO_IDN:CO_IDN + 128]),
                   reads=[gst_r, cst_r], writes=[psr[0]])
            op("dve", lambda e: e.tensor_copy(gains[:].rearrange("p a k -> p (a k)"), ps[:, 0, 0:256]),
               reads=[psr[0]], writes=[gains_r])
            op("dve", lambda e: e.tensor_copy(rgn[:], ps[:, 0, 256:288]), reads=[psr[0]], writes=[gains_r])
            tk.barrier()

        def gain(gi, l, k):
            return gains[:, gi * DEPTH + l, k:k + 1]

        lam = sb("lam", [128, DEPTH * 16], F32)
        lam_r = Res()
        dma("sp", lam[:], rld.rearrange("l d h -> (l d h)").partition_broadcast(128), writes=[lam_r])
        op("act", lambda e: e.activation(lam[:], lam[:], AF.Exp), reads=[lam_r], writes=[lam_r])
        op("dve", lambda e: e.tensor_scalar(lam[:], lam[:], -1.0, None, ALU.mult), reads=[lam_r], writes=[lam_r])

        if want("bias"):
            with ExitStack() as es:
                tab = sb("tab", [33, 16], F32, es)
                oh = sb("oh", [33, 3, 384], F32, es)
                ust = sb("ust", [16, 3, 384], F32, es)
                r1 = Res()
                op("dve", lambda e: e.memset(tab[32:33, :], NEG), writes=[r1])
                dma("sp", tab[0:32, :], relb[:, :], writes=[r1])
                dma("sp", oh[:], onehot.rearrange("d b x -> b d x"), writes=[r1])
                for d3 in range(3):
                    op("pe", lambda e: e.matmul(ps[0:16, d3, 0:384], tab[:, :], oh[:, d3, :], start=True, stop=True),
                       reads=[r1], writes=[psr[d3]])
                    op("dve", lambda e: e.tensor_copy(ust[:, d3, :], ps[0:16, d3, 0:384]), reads=[psr[d3]], writes=[r1])
                for d3 in range(3):
                    dma("sp", U[d3], ust[:, d3, :].unsqueeze(1).to_broadcast([16, 129, 384]), reads=[r1])
            tk.barrier()

        if want("xt"):
            with ExitStack() as es:
                xin = [sb(f"xin{i}", [128, D], F32, es) for i in range(2)]
                xin_r = [Res() for _ in range(2)]
                xst = [sb(f"xst{i}", [128, KC, 128], F32, es) for i in range(2)]
                xst_r = [Res() for _ in range(2)]
                for s in range(NTOK // 128):
                    b = s % 2
                    dma("sp", xin[b][:], x_in[s * 128:(s + 1) * 128, :], writes=[xin_r[b]])
                    for k4 in range(8):
                        bank = (s * 8 + k4) % 4
                        for kk in range(4):
                            k = k4 * 4 + kk
                            op("pe", lambda e: e.transpose(ps[:, bank, kk * 128:(kk + 1) * 128],
                                                           xin[b][:, k * 128:(k + 1) * 128], idn32),
                               reads=[xin_r[b], cst_r], writes=[psr[bank]], inc=(kk == 3))
                        eng = "dve" if k4 % 2 == 0 else "act"
                        if eng == "dve":
                            op("dve", lambda e: e.tensor_copy(xst[b][:, k4 * 4:(k4 + 1) * 4, :],
                                                              ps[:, bank, :].rearrange("p (a b) -> p a b", a=4)),
                               reads=[psr[bank]], writes=[xst_r[b]])
                        else:
                            op("act", lambda e: e.activation(xst[b][:, k4 * 4:(k4 + 1) * 4, :],
                                                             ps[:, bank, :].rearrange("p (a b) -> p a b", a=4), AF.Copy),
                               reads=[psr[bank]], writes=[xst_r[b]])
                    for hh in range(2):
                        dma("sp", XT[hh * 16:(hh + 1) * 16, :, s * 128:(s + 1) * 128].rearrange("k p t -> p k t"),
                            xst[b][:, hh * 16:(hh + 1) * 16, :], reads=[xst_r[b]])
            tk.barrier()

        for l in range(nlayers):
            last = (l == DEPTH - 1)
            if want("A"):
                with ExitStack() as es:
                    big = sb("big", [128, KC, T], F32, es)
                    big_r = [Res() for _ in range(KC)]
                    hT = sb("hT", [128, KC, T], BF16, es)
                    hT_r = Res()
                    wb = [sb(f"wb{i}", [128, KC, 256], BF16, es) for i in range(4)]
                    wb_r = [[Res(), Res()] for _ in range(4)]
                    rt = sb("rt", [128, 4, T], F32, es)
                    rt_r = Res()
                    sq = [sb(f"sq{i}", [128, T], BF16, es) for i in range(2)]
                    sq_r = [Res() for _ in range(2)]
                    rstd = sb("rstd", [128, T], F32, es)
                    rstd_r = Res()
                    ost = [sb(f"ost{i}", [128, T], BF16, es) for i in range(4)]
                    ost_r = [Res() for _ in range(4)]
                    qb = [sb(f"qb{i}", [128, T], BF16, es) for i in range(2)]
                    qb_r = [Res() for _ in range(2)]
                    t1 = [sb(f"t1{i}", [128, T], F32, es) for i in range(2)]
                    t1_r = [Res() for _ in range(2)]
                    t2 = [sb(f"t2{i}", [128, T], F32, es) for i in range(2)]
                    t2_r = [Res() for _ in range(2)]
                    NG = INW // 256
                    a_groups = dbg.get("A_groups", None)
                    glist = list(range(NG)) if a_groups is None else list(a_groups)
                    NG = len(glist)
                    NTA = dbg.get("A_tiles", NT)

                    def issue_w(idx):
                        tt, gi_ = divmod(idx, NG)
                        g = glist[gi_]
                        b = idx % 4
                        for hh in range(2):
                            dma("pool", wb[b][:, hh * 16:(hh + 1) * 16, :],
                                w_in[l, hh * 2048:(hh + 1) * 2048, g * 256:(g + 1) * 256].rearrange("(k p) n -> p k n", p=128),
                                writes=[wb_r[b][hh]])

                    total = NTA * NG
                    nissued = 0
                    cnt = dict(ps=0, ost=0, rot=0, sq=0)
                    for tt in range(NTA):
                        tok = slice(tt * T, (tt + 1) * T)
                        while nissued < min(total, tt * NG + 3):
                            issue_w(nissued)
                            nissued += 1
                        for q4 in range(4):
                            dma("sp", big[:, q4 * 8:(q4 + 1) * 8, :],
                                XT[q4 * 8:(q4 + 1) * 8, :, tok].rearrange("k p t -> p k t"),
                                writes=big_r[q4 * 8:(q4 + 1) * 8])
                        dma("sp", rt[:], rot[:, :, tok].rearrange("f p t -> p f t"), writes=[rt_r])
                        for k in range(KC):
                            b = cnt["sq"] % 2
                            cnt["sq"] += 1
                            op("act", lambda e: e.activation(sq[b][:], big[:, k, :], AF.Square),
                               reads=[big_r[k]], writes=[sq_r[b]])
                            op("pe", lambda e: e.matmul(ps[:, 6, :], om4096[:], sq[b][:], start=(k == 0), stop=(k == KC - 1)),
                               reads=[sq_r[b], k_r], writes=[psr[6]])
                        op("act", lambda e: e.activation(rstd[:], ps[:, 6, :], AF.Sqrt, bias=epsc[:, 0:1]),
                           reads=[psr[6], k_r], writes=[rstd_r])
                        op("dve", lambda e: e.reciprocal(rstd[:], rstd[:]), reads=[rstd_r], writes=[rstd_r])
                        for k in range(KC):
                            en = "dve"
                            op(en, lambda e: e.scalar_tensor_tensor(hT[:, k, :], big[:, k, :], gain(0, l, k), rstd[:],
                                                                    ALU.mult, ALU.mult),
                               reads=[big_r[k], rstd_r, gains_r], writes=[hT_r])
                        for gi_ in range(NG):
                            g = glist[gi_]
                            idx = tt * NG + gi_
                            while nissued < min(total, idx + 3):
                                issue_w(nissued)
                                nissued += 1
                            b = idx % 4
                            kind = _col_kind(g)
                            W = wb[b]
                            amode = dbg.get("A_mode", 3)
                            if amode == 0:
                                op("dve", lambda e: e.tensor_copy(ost[0][:, 0:256], W[:, 3, :]), reads=wb_r[b], writes=[ost_r[0]])
                                continue
                            for c2 in range(2):
                                col0 = g * 256 + c2 * 128
                                bank = cnt["ps"] % 4
                                cnt["ps"] += 1
                                if kind in ("rv", "av"):
                                    for j2 in range(2):
                                        j = c2 * 2 + j2
                                        for k in range(KC):
                                            op("pe", lambda e: e.matmul(ps[:, bank, j2 * 256:(j2 + 1) * 256],
                                                                        hT[:, k, j * 128:(j + 1) * 128], W[:, k, :],
                                                                        start=(k == 0), stop=(k == KC - 1)),
                                               reads=[hT_r, wb_r[b][k // 16]], writes=[psr[bank]], inc=(k == KC - 1))
                                    o = cnt["ost"] % 4
                                    cnt["ost"] += 1
                                    en = "act" if cnt["ost"] % 2 == 0 else "dve"
                                    if en == "act":
                                        op("act", lambda e: e.activation(ost[o][:], ps[:, bank, :], AF.Copy),
                                           reads=[psr[bank]], writes=[ost_r[o]])
                                    else:
                                        op("dve", lambda e: e.tensor_copy(ost[o][:], ps[:, bank, :]),
                                           reads=[psr[bank]], writes=[ost_r[o]])
                                    vc0 = (g * 256 - 2048) if kind == "rv" else (g * 256 - 10240 + 2048)
                                    r0 = tt * T + c2 * 256
                                    dma("sp", VT[r0:r0 + 256, vc0:vc0 + 256].rearrange("(j p) c -> p j c", p=128),
                                        ost[o][:].rearrange("p (j c) -> p j c", j=2), reads=[ost_r[o]])
                                    continue
                                for k in range(KC):
                                    op("pe", lambda e: e.matmul(ps[:, bank, :], W[:, k, c2 * 128:(c2 + 1) * 128], hT[:, k, :],
                                                                start=(k == 0), stop=(k == KC - 1)),
                                       reads=[hT_r, wb_r[b][k // 16]], writes=[psr[bank]], inc=(k == KC - 1))
                                if amode == 1:
                                    continue
                                o = cnt["ost"] % 4
                                cnt["ost"] += 1
                                if kind in ("rq", "rk"):
                                    fi = 0 if kind == "rq" else 2
                                    ch = col0 // 128
                                    r = cnt["rot"] % 2
                                    cnt["rot"] += 1
                                    op("act", lambda e: e.activation(qb[r][:], ps[:, bank, :], AF.Copy),
                                       reads=[psr[bank]], writes=[qb_r[r]])
                                    op("dve", lambda e: e.tensor_tensor(t1[r][:], ps[:, bank, :], rt[:, fi, :], ALU.mult),
                                       reads=[psr[bank], rt_r, qb_r[r]], writes=[t1_r[r]])
                                    if amode == 4:
                                        continue
                                    op("pe", lambda e: e.matmul(ps[:, 4 + r, :], swp_bf[:], qb[r][:], start=True, stop=True),
                                       reads=[qb_r[r], k_r], writes=[psr[4 + r]])
                                    op("dve", lambda e: e.tensor_tensor(t2[r][:], ps[:, 4 + r, :], rt[:, fi + 1, :], ALU.mult),
                                       reads=[psr[4 + r], rt_r], writes=[t2_r[r]])
                                    if amode == 5:
                                        continue
                                    op("dve", lambda e: e.tensor_tensor(ost[o][:], t1[r][:], t2[r][:], ALU.add),
                                       reads=[t1_r[r], t2_r[r]], writes=[ost_r[o]])
                                    if amode == 6:
                                        continue
                                elif kind == "rg":
                                    ch = 48 + (col0 - 4096) // 128
                                    op("act", lambda e: e.activation(ost[o][:], ps[:, bank, :], AF.Silu),
                                       reads=[psr[bank]], writes=[ost_r[o]])
                                elif kind == "aq":
                                    ch = 16 + (col0 - 6144) // 128
                                    op("act", lambda e: e.mul(ost[o][:], ps[:, bank, :], float(128 ** -0.5)),
                                       reads=[psr[bank]], writes=[ost_r[o]])
                                else:
                                    ch = 32 + (col0 - 8192) // 128
                                    op("dve", lambda e: e.tensor_copy(ost[o][:], ps[:, bank, :]),
                                       reads=[psr[bank]], writes=[ost_r[o]])
                                dma("sp", FT[ch, :, tok], ost[o][:], reads=[ost_r[o]])
                tk.barrier()

            if want("B1"):
                with ExitStack() as es:
                    QT = sb("rQT", [128, NTOK], BF16, es)
                    KT = sb("rKT", [128, NTOK], BF16, es)
                    GT = sb("rGT", [128, 2, NTOK], BF16, es)
                    Vr = sb("rV", [128, NCH, 256], BF16, es)
                    Yr = sb("rY", [128, 2, NTOK], BF16, es)
                    QfA = sb("rQfA", [128, NTOK], BF16, es)
                    QbA = sb("rQbA", [128, NTOK], BF16, es)
                    Kf_all = sb("rKf", [128, NCH, 128], BF16, es)
                    Kb_all = sb("rKb", [128, NCH, 128], BF16, es)
                    Sb_all = sb("rSb", [128, NCH, 256], BF16, es)
                    q_r, kk_r, g_r, v_r = Res(), Res(), Res(), Res()
                    Yr_r, Kf_r, Kb_r, Sb_r, QfA_r, QbA_r = Res(), Res(), Res(), Res(), Res(), Res()
                    maskT = sb("maskT", [128, 128], F32, es)
                    mtmp = sb("mtmp", [128, 128], F32, es)
                    qdf = sb("qdf", [128, 128], F32, es)
                    qdb = sb("qdb", [128, 128], F32, es)
                    hc = sb("hc", [128, 8], F32, es)
                    kdk = sb("kdk", [128, 4, NCH], F32, es)
                    hc_r = Res()
                    S32 = sb("S32", [128, 256], F32, es)
                    S_r = Res()
                    Sf_bf = [sb(f"Sfbf{i}", [128, 256], BF16, es) for i in range(2)]
                    Sf_r = [Res() for _ in range(2)]
                    St = [sb(f"St{i}", [128, 128], BF16, es) for i in range(2)]
                    St_r = [Res() for _ in range(2)]
                    ysq = [sb(f"ysq{i}", [128, 2, 128], BF16, es) for i in range(2)]
                    ysq_r = [Res() for _ in range(2)]
                    yrs = [sb(f"yrs{i}", [128, 128], F32, es) for i in range(2)]
                    yrs_r = [Res() for _ in range(2)]
                    yt = [sb(f"yt{i}", [128, 2, 128], F32, es) for i in range(2)]
                    yt_r = [Res() for _ in range(2)]
                    keepf_sh = cst[:, CO_KEEPF + 1:CO_KEEPF + NCH]
                    keepb_sh = cst[:, CO_KEEPB:CO_KEEPB + NCH - 1]
                    for h in range(8):
                        lf = lam[:, l * 16 + h:l * 16 + h + 1]
                        lb = lam[:, l * 16 + 8 + h:l * 16 + 8 + h + 1]
                        dma("sp", QT[:], FT[h, :, :], writes=[q_r])
                        dma("sp", KT[:], FT[8 + h, :, :], writes=[kk_r])
                        dma("sp", GT[:], FT[48 + 2 * h:48 + 2 * h + 2, :, :].rearrange("e p t -> p e t"), writes=[g_r])
                        for hh in range(2):
                            dma("sp", Vr[:, hh * 16:(hh + 1) * 16, :],
                                VT[hh * 2048:(hh + 1) * 2048, h * 256:(h + 1) * 256].rearrange("(n p) c -> p n c", p=128), writes=[v_r])
                        op("dve", lambda e: e.tensor_scalar(mtmp[:], cst[:, CO_DPOS:CO_DPOS + 128], lf, None, ALU.mult),
                           reads=[cst_r, lam_r], writes=[hc_r])
                        op("dve", lambda e: e.scalar_tensor_tensor(mtmp[:], cst[:, CO_DNEG:CO_DNEG + 128], lb, mtmp[:],
                                                                   ALU.mult, ALU.add),
                           reads=[cst_r, lam_r, hc_r], writes=[hc_r])
                        op("act", lambda e: e.activation(maskT[:], mtmp[:], AF.Exp), reads=[hc_r], writes=[hc_r])
                        op("act", lambda e: e.activation(qdf[:], cst[:, CO_IP1:CO_IP1 + 128], AF.Exp, scale=lf),
                           reads=[cst_r, lam_r], writes=[hc_r])
                        op("act", lambda e: e.activation(qdb[:], cst[:, CO_CMI:CO_CMI + 128], AF.Exp, scale=lb),
                           reads=[cst_r, lam_r], writes=[hc_r])
                        op("act", lambda e: e.activation(hc[:, 0:1], cst[:, CO_CM1J:CO_CM1J + 1], AF.Exp, scale=lf),
                           reads=[cst_r, lam_r], writes=[hc_r])
                        op("act", lambda e: e.activation(hc[:, 1:2], cst[:, CO_JCOL:CO_JCOL + 1], AF.Exp, scale=lb),
                           reads=[cst_r, lam_r], writes=[hc_r])
                        op("act", lambda e: e.activation(hc[:, 2:3], cst[:, CO_C128:CO_C128 + 1], AF.Exp, scale=lf),
                           reads=[cst_r, lam_r], writes=[hc_r])
                        op("act", lambda e: e.activation(hc[:, 3:4], cst[:, CO_C128:CO_C128 + 1], AF.Exp, scale=lb),
                           reads=[cst_r, lam_r], writes=[hc_r])
                        op("dve", lambda e: e.memset(kdk[:], 0.0), reads=[hc_r], writes=[hc_r])
                        op("dve", lambda e: e.tensor_scalar(kdk[:, 0, 0:NCH - 1], keepf_sh, hc[:, 0:1], None, ALU.mult),
                           reads=[hc_r, cst_r], writes=[hc_r])
                        op("dve", lambda e: e.tensor_scalar(kdk[:, 1, 1:NCH], keepb_sh, hc[:, 1:2], None, ALU.mult),
                           reads=[hc_r, cst_r], writes=[hc_r])
                        op("dve", lambda e: e.tensor_scalar(kdk[:, 2, 0:NCH - 1], keepf_sh, hc[:, 2:3], None, ALU.mult),
                           reads=[hc_r, cst_r], writes=[hc_r])
                        op("dve", lambda e: e.tensor_scalar(kdk[:, 3, 1:NCH], keepb_sh, hc[:, 3:4], None, ALU.mult),
                           reads=[hc_r, cst_r], writes=[hc_r])
                        op("dve", lambda e: e.tensor_tensor(QfA[:].rearrange("p (n i) -> p n i", i=C), QT[:].rearrange("p (n i) -> p n i", i=C),
                                                             qdf[:].unsqueeze(1).to_broadcast([128, NCH, C]), ALU.mult),
                           reads=[q_r, hc_r], writes=[QfA_r])
                        op("dve", lambda e: e.tensor_tensor(QbA[:].rearrange("p (n i) -> p n i", i=C), QT[:].rearrange("p (n i) -> p n i", i=C),
                                                             qdb[:].unsqueeze(1).to_broadcast([128, NCH, C]), ALU.mult),
                           reads=[q_r, hc_r], writes=[QbA_r])
                        for e2 in range(2):
                            gcol = rgn[:, (l * 8 + h) * 2 + e2:(l * 8 + h) * 2 + e2 + 1]
                            op("act", lambda e: e.mul(GT[:, e2, :], GT[:, e2, :], gcol),
                               reads=[g_r, gains_r], writes=[g_r])
                        for n in range(NCH):
                            cs = slice(n * C, (n + 1) * C)
                            i2 = n % 2
                            pT = ps[:, 0 + i2, 0:64].bitcast(BF16)
                            op("pe", lambda e: e.transpose(pT, KT[:, cs], idn_bf[:]),
                               reads=[kk_r, k_r], writes=[psr[0 + i2]])
                            if n < NCH - 1:
                                op("act", lambda e: e.mul(Kf_all[:, n, :], pT, kdk[:, 0, n:n + 1]),
                                   reads=[psr[0 + i2], hc_r], writes=[Kf_r])
                            if n > 0:
                                op("act", lambda e: e.mul(Kb_all[:, n, :], pT, kdk[:, 1, n:n + 1]),
                                   reads=[psr[0 + i2], hc_r], writes=[Kb_r])
                        op("dve", lambda e: e.memset(S32[:], 0.0), writes=[S_r])
                        op("pool", lambda e: e.memset(Sb_all[:, NCH - 1, :], 0.0), writes=[Sb_r])
                        for n in range(NCH - 1, 0, -1):
                            i2 = n % 2
                            op("pe", lambda e: e.matmul(ps[:, 2 + i2, 0:256], Kb_all[:, n, :], Vr[:, n, :], start=True, stop=True),
                               reads=[Kb_r, v_r], writes=[psr[2 + i2]])
                            op("dve", lambda e: e.scalar_tensor_tensor(S32[:], S32[:], kdk[:, 3, n:n + 1], ps[:, 2 + i2, 0:256],
                                                                       ALU.mult, ALU.add),
                               reads=[S_r, hc_r, psr[2 + i2]], writes=[S_r])
                            op("act", lambda e: e.activation(Sb_all[:, n - 1, :], S32[:], AF.Copy), reads=[S_r], writes=[Sb_r])
                        op("dve", lambda e: e.memset(S32[:], 0.0), writes=[S_r])
                        op("pool", lambda e: e.memset(Sf_bf[0][:], 0.0), writes=[Sf_r[0]])

                        def stA(n):
                            cs = slice(n * C, (n + 1) * C)
                            i2 = n % 2
                            op("pe", lambda e: e.matmul(ps[:, i2, 0:128], KT[:, cs], QT[:, cs], start=True, stop=True),
                               reads=[kk_r, q_r], writes=[psr[i2]])
                            op("dve", lambda e: e.tensor_tensor(St[i2][:], ps[:, i2, 0:128], maskT[:], ALU.mult),
                               reads=[psr[i2], hc_r], writes=[St_r[i2]])

                        def stB(n):
                            cs = slice(n * C, (n + 1) * C)
                            i2 = n % 2
                            yb = 5 if i2 == 0 else 7
                            for e2 in range(2):
                                es_ = slice(e2 * 128, (e2 + 1) * 128)
                                po = ps[:, yb, e2 * 128:(e2 + 1) * 128]
                                op("pe", lambda e: e.matmul(po, Vr[:, n, es_], St[i2][:], start=True, stop=False),
                                   reads=[v_r, St_r[i2]], writes=[psr[yb]], inc=False)
                                op("pe", lambda e: e.matmul(po, Sf_bf[i2][:, es_], QfA[:, cs], start=False, stop=False),
                                   reads=[Sf_r[i2], QfA_r], writes=[psr[yb]], inc=False)
                                op("pe", lambda e: e.matmul(po, Sb_all[:, n, es_], QbA[:, cs], start=False, stop=True),
                                   reads=[Sb_r, QbA_r], writes=[psr[yb]], inc=(e2 == 1))
                            if n < NCH - 1:
                                op("pe", lambda e: e.matmul(ps[:, 2 + i2, 0:256], Kf_all[:, n, :], Vr[:, n, :], start=True, stop=True),
                                   reads=[Kf_r, v_r], writes=[psr[2 + i2]])
                                op("dve", lambda e: e.scalar_tensor_tensor(S32[:], S32[:], kdk[:, 2, n:n + 1], ps[:, 2 + i2, 0:256],
                                                                           ALU.mult, ALU.add),
                                   reads=[S_r, hc_r, psr[2 + i2]], writes=[S_r])
                                op("act", lambda e: e.activation(Sf_bf[1 - i2][:], S32[:], AF.Copy), reads=[S_r], writes=[Sf_r[1 - i2]])
                            op("act", lambda e: e.activation(ysq[i2][:], ps[:, yb, 0:256].rearrange("p (a b) -> p a b", a=2), AF.Square),
                               reads=[psr[yb]], writes=[ysq_r[i2]])

                        def stC(n):
                            cs = slice(n * C, (n + 1) * C)
                            i2 = n % 2
                            yb = 5 if i2 == 0 else 7
                            for e2 in range(2):
                                op("pe", lambda e: e.matmul(ps[:, 6, 0:128], om256[:], ysq[i2][:, e2, :], start=(e2 == 0), stop=(e2 == 1)),
                                   reads=[ysq_r[i2], k_r], writes=[psr[6]], inc=(e2 == 1))
                            op("act", lambda e: e.activation(yrs[i2][:], ps[:, 6, 0:128], AF.Sqrt, bias=epsc[:, 0:1]),
                               reads=[psr[6], k_r], writes=[yrs_r[i2]])
                            op("dve", lambda e: e.reciprocal(yrs[i2][:], yrs[i2][:]), reads=[yrs_r[i2]], writes=[yrs_r[i2]])
                            op("dve", lambda e: e.tensor_tensor(yt[i2][:], ps[:, yb, 0:256].rearrange("p (a b) -> p a b", a=2),
                                                                yrs[i2][:].unsqueeze(1).to_broadcast([128, 2, 128]), ALU.mult),
                               reads=[psr[yb], yrs_r[i2]], writes=[yt_r[i2]])
                            op("pool", lambda e: e.tensor_tensor(Yr[:, :, cs], yt[i2][:], GT[:, :, cs], ALU.mult),
                               reads=[yt_r[i2], g_r], writes=[Yr_r])

                        for n in range(NCH + 2):
                            if n < NCH:
                                stA(n)
                            if 0 <= n - 1 < NCH:
                                stB(n - 1)
                            if 0 <= n - 2 < NCH:
                                stC(n - 2)
                        dma("sp", YT[2 * h:2 * h + 2, :, :].rearrange("e p t -> p e t"), Yr[:], reads=[Yr_r])
                tk.barrier()

            if want("B2"):
                with ExitStack() as es:
                    QT = sb("aQT", [128, NTOK], BF16, es)
                    KT = sb("aKT", [128, NTOK], BF16, es)
                    Vd = [sb(f"aV{i}", [128, 32, 128], BF16, es) for i in range(3)]
                    acc = sb("aacc", [128, 2, NTOK], F32, es)
                    yo = sb("ayo", [128, NTOK], BF16, es)
                    Bx = sb("Bx", [128, 3, 3, 256], F32, es)
                    q_r, kk_r, v_r = Res(), Res(), Res()
                    b_r = Res()
                    acc_r = [Res() for _ in range(32)]
                    yo_r = Res()
                    NB = 4
                    SKEW = 3
                    scs = [sb(f"scs{i}", [128, 256], F32, es) for i in range(NB)]
                    scs_r = [Res() for _ in range(NB)]
                    Pm = [sb(f"Pm{i}", [128, 256], BF16, es) for i in range(NB)]
                    Pm_r = [Res() for _ in range(NB)]
                    negc = cst[:, CO_NEGC:CO_NEGC + 1]
                    for h in range(16):
                        dma("sp", QT[:], FT[16 + h, :, :], writes=[q_r])
                        dma("sp", KT[:], FT[32 + h, :, :], writes=[kk_r])
                        vsrc = VT[:, 2048 + h * 128:2048 + (h + 1) * 128]
                        for hh in range(2):
                            dma("sp", Vd[0][:, hh * 16:(hh + 1) * 16, :],
                                VT[hh * 2048:(hh + 1) * 2048, 2048 + h * 128:2048 + (h + 1) * 128].rearrange("(m p) c -> p m c", p=128),
                                writes=[v_r])
                        for bi, dd in ((1, 4), (2, 16)):
                            v4 = vsrc.rearrange("(m p g) c -> g p m c", p=128, g=dd)
                            mpg = 32 // dd
                            for g in range(dd):
                                dma("sp", Vd[bi][:, g * mpg:(g + 1) * mpg, :], v4[g], writes=[v_r])
                        for bi in range(3):
                            base = (bi * 16 + h) * 129 * 384
                            for va in range(3):
                                dma("sp", Bx[:, va, bi, :], AP(U_t, base + 127, [[383, 128], [1, 256]]), writes=[b_r])
                        op("pool", lambda e: e.tensor_scalar(Bx[:, 1, :, 192:256], Bx[:, 1, :, 192:256], negc, None, ALU.add),
                           reads=[b_r, cst_r], writes=[b_r])
                        op("pool", lambda e: e.tensor_scalar(Bx[:, 2, :, 0:64], Bx[:, 2, :, 0:64], negc, None, ALU.add),
                           reads=[b_r, cst_r], writes=[b_r])
                        op("dve", lambda e: e.memset(acc[:], 0.0), reads=acc_r, writes=acc_r)
                        units = []
                        for bi, dd in ((0, 1), (1, 4), (2, 16)):
                            mpg = 32 // dd
                            for g in range(dd):
                                for m in range(mpg):
                                    units.append((bi, dd, g, m, mpg))

                        def rcols(dd, g, l0, n):
                            s0 = g + dd * l0
                            return slice(s0, s0 + dd * (n - 1) + 1, dd)

                        def qrange(m, mpg):
                            lo = 64 if m == 0 else 0
                            hi = 192 if m == mpg - 1 else 256
                            return lo, hi

                        def stage1(u, ui):
                            bi, dd, g, m, mpg = u
                            i3 = ui % NB
                            bank = i3
                            lo, hi = qrange(m, mpg)
                            qsl = rcols(dd, g, 128 * m - 64 + lo, hi - lo)
                            op("pe", lambda e: e.matmul(ps[:, bank, lo:hi], KT[:, rcols(dd, g, 128 * m, 128)], QT[:, qsl],
                                                        start=True, stop=True),
                               reads=[kk_r, q_r], writes=[psr[bank]])
                            mid = mpg // 2
                            va = 1 if m == mid - 1 else (2 if m == mid else 0)
                            op("act", lambda e: e.activation(scs[i3][:, lo:hi], ps[:, bank, lo:hi], AF.Copy),
                               reads=[psr[bank]], writes=[scs_r[i3]])
                            op("pool", lambda e: e.tensor_tensor(scs[i3][:, lo:hi], scs[i3][:, lo:hi], Bx[:, va, bi, lo:hi], ALU.add),
                               reads=[scs_r[i3], b_r], writes=[scs_r[i3]])
                            op("act", lambda e: e.activation(Pm[i3][:, lo:hi], scs[i3][:, lo:hi], AF.Exp),
                               reads=[scs_r[i3]], writes=[Pm_r[i3]])

                        def stage2(u, ui):
                            bi, dd, g, m, mpg = u
                            i3 = ui % NB
                            bank = 4 + ui % 2
                            lo, hi = qrange(m, mpg)
                            t0 = g * mpg + m
                            op("pe", lambda e: e.matmul(ps[:, bank, lo:hi], Vd[bi][:, t0, :], Pm[i3][:, lo:hi], start=True, stop=True),
                               reads=[v_r, Pm_r[i3]], writes=[psr[bank]], inc=False)
                            op("pe", lambda e: e.matmul(ps[:, bank, 256 + lo:256 + hi], ones_bf[:], Pm[i3][:, lo:hi], start=True, stop=True),
                               reads=[k_r, Pm_r[i3]], writes=[psr[bank]])
                            l0 = 128 * m - 64 + lo
                            cols = rcols(dd, g, l0, hi - lo)
                            c_lo = g + dd * l0
                            c_hi = g + dd * (l0 + hi - lo - 1)
                            ar = [acc_r[b_] for b_ in range(c_lo // 128, c_hi // 128 + 1)]
                            pv = ps[:, bank, :].rearrange("p (a b) -> p a b", a=2)[:, :, lo:hi]
                            op("dve", lambda e: e.tensor_tensor(acc[:, :, cols], pv, acc[:, :, cols], ALU.add),
                               reads=[psr[bank]] + ar, writes=ar)

                        nu = len(units)
                        for ui in range(nu + SKEW):
                            if ui < nu:
                                stage1(units[ui], ui)
                            if 0 <= ui - SKEW < nu:
                                stage2(units[ui - SKEW], ui - SKEW)
                        op("act", lambda e: e.activation(acc[:, 1, :], acc[:, 1, :], AF.Ln), reads=acc_r, writes=acc_r)
                        op("act", lambda e: e.activation(acc[:, 1, :], acc[:, 1, :], AF.Exp, scale=-1.0), reads=acc_r, writes=acc_r)
                        op("dve", lambda e: e.tensor_tensor(yo[:], acc[:, 0, :], acc[:, 1, :], ALU.mult), reads=acc_r, writes=[yo_r])
                        dma("sp", YT[16 + h, :, :], yo[:], reads=[yo_r])
                tk.barrier()

            if want("C"):
                with ExitStack() as es:
                    big = sb("cbig", [128, KC, T], F32, es)
                    big_r = [Res() for _ in range(KC)]
                    aT = sb("caT", [128, KC, T], BF16, es)
                    aT_r = Res()
                    wb = [sb(f"cwb{i}", [128, 8192], BF16, es) for i in range(4)]
                    wb_r = [[Res(), Res()] for _ in range(4)]
                    uT = [sb(f"uT{i}", [128, 2, T], BF16, es) for i in range(2)]
                    uT_r = [Res() for _ in range(2)]
                    ur = [sb(f"ur{i}", [128, T], F32, es) for i in range(2)]
                    ur_r = [Res() for _ in range(2)]
                    xs = [sb(f"cxs{i}", [128, T], F32, es) for i in range(3)]
                    xs_r = [Res() for _ in range(3)]
                    sq = [sb(f"csq{i}", [128, T], BF16, es) for i in range(4)]
                    sq_r = [Res() for _ in range(4)]
                    rstd = sb("crstd", [128, T], F32, es)
                    rstd_r = Res()
                    tm = [sb(f"ctm{i}", [128, T], F32, es) for i in range(2)]
                    tm_r = [Res() for _ in range(2)]
                    if last:
                        orow = [sb(f"orow{i}", [128, 1024], F32, es) for i in range(2)]
                        orow_r = [Res() for _ in range(2)]
                    x1_r = [Res() for _ in range(KC)]
                    xt_r = [Res() for _ in range(KC)]
                    NOG = D // 256
                    NFG = DFF // 256
                    stream = [("o", i) for i in range(NOG)]
                    for fg in range(NFG):
                        stream.append(("u", fg))
                        stream.append(("d", fg))
                    SL = len(stream)
                    total = NT * SL
                    st = dict(n=0)

                    def issue_w(idx):
                        kind, i = stream[idx % SL]
                        b = idx % 4
                        if kind == "o":
                            dst = wb[b][:].rearrange("p (k n) -> p k n", k=KC)
                            for hh in range(2):
                                dma("pool", dst[:, hh * 16:(hh + 1) * 16, :],
                                    w_out[l, hh * 2048:(hh + 1) * 2048, i * 256:(i + 1) * 256].rearrange("(k p) n -> p k n", p=128),
                                    writes=[wb_r[b][hh]])
                        elif kind == "u":
                            dst = wb[b][:].rearrange("p (k n) -> p k n", k=KC)
                            for hh in range(2):
                                dma("pool", dst[:, hh * 16:(hh + 1) * 16, :],
                                    w_up[l, hh * 2048:(hh + 1) * 2048, i * 256:(i + 1) * 256].rearrange("(k p) n -> p k n", p=128),
                                    writes=[wb_r[b][hh]])
                        else:
                            dst = wb[b][:].rearrange("p (f n) -> p f n", f=2)
                            for hh in range(2):
                                dma("pool", dst[:, hh, :], w_down[l, i * 256 + hh * 128:i * 256 + (hh + 1) * 128, :],
                                    writes=[wb_r[b][hh]])

                    def prefetch(upto):
                        while st["n"] < min(total, upto):
                            issue_w(st["n"])
                            st["n"] += 1

                    cnt = dict(ps=0, sq=0, xs=0, tm=0, pd=0, u=0, orow=0)

                    pend = []

                    def ss_square(k):
                        b = cnt["sq"] % 4
                        cnt["sq"] += 1
                        op("act", lambda e: e.activation(sq[b][:], big[:, k, :], AF.Square),
                           reads=[big_r[k]], writes=[sq_r[b]])
                        pend.append((k, b))

                    def ss_flush(keep=0):
                        while len(pend) > keep:
                            k, b = pend.pop(0)
                            op("pe", lambda e: e.matmul(ps[:, 6, :], om4096[:], sq[b][:], start=(k == 0), stop=(k == KC - 1)),
                               reads=[sq_r[b], k_r], writes=[psr[6]])

                    def ss_rstd():
                        ss_flush(0)
                        op("act", lambda e: e.activation(rstd[:], ps[:, 6, :], AF.Sqrt, bias=epsc[:, 0:1]),
                           reads=[psr[6], k_r], writes=[rstd_r])
                        op("dve", lambda e: e.reciprocal(rstd[:], rstd[:]), reads=[rstd_r], writes=[rstd_r])

                    for tt in range(NT):
                        tok = slice(tt * T, (tt + 1) * T)
                        base = tt * SL
                        prefetch(base + 3)
                        for q4 in range(4):
                            dma("sp", aT[:, q4 * 8:(q4 + 1) * 8, :],
                                YT[q4 * 8:(q4 + 1) * 8, :, tok].rearrange("k p t -> p k t"), writes=[aT_r])
                        for og in range(NOG):
                            idx = base + og
                            prefetch(idx + 3)
                            b = idx % 4
                            W = wb[b][:].rearrange("p (k n) -> p k n", k=KC)
                            for c2 in range(2):
                                oc = og * 2 + c2
                                bank = cnt["ps"] % 2
                                cnt["ps"] += 1
                                for k in range(KC):
                                    op("pe", lambda e: e.matmul(ps[:, bank, :], W[:, k, c2 * 128:(c2 + 1) * 128], aT[:, k, :],
                                                                start=(k == 0), stop=(k == KC - 1)),
                                       reads=[aT_r, wb_r[b][k // 16]], writes=[psr[bank]], inc=(k == KC - 1))
                                ss_flush(1)
                                op("dve", lambda e: e.tensor_copy(big[:, oc, :], ps[:, bank, :]),
                                   reads=[psr[bank]], writes=[big_r[oc]])
                                ss_square(oc)
                        ss_rstd()
                        for oc in range(KC):
                            xb = cnt["xs"] % 3
                            cnt["xs"] += 1
                            dma("sp", xs[xb][:], XT[oc, :, tok], reads=[xt_r[oc]], writes=[xs_r[xb]])
                            tb = cnt["tm"] % 2
                            cnt["tm"] += 1
                            op("dve", lambda e: e.scalar_tensor_tensor(tm[tb][:], big[:, oc, :], gain(1, l, oc), rstd[:],
                                                                       ALU.mult, ALU.mult),
                               reads=[big_r[oc], gains_r, rstd_r], writes=[tm_r[tb]])
                            op("pool", lambda e: e.tensor_tensor(big[:, oc, :], tm[tb][:], xs[xb][:], ALU.add),
                               reads=[tm_r[tb], xs_r[xb]], writes=[big_r[oc]])
                            dma("sp", X1T[oc, :, tok], big[:, oc, :], reads=[big_r[oc]], writes=[x1_r[oc]])
                            ss_square(oc)
                            ss_flush(2)
                        ss_rstd()
                        for k in range(KC):
                            en = "dve"
                            op(en, lambda e: e.scalar_tensor_tensor(aT[:, k, :], big[:, k, :], gain(2, l, k), rstd[:],
                                                                    ALU.mult, ALU.mult),
                               reads=[big_r[k], rstd_r, gains_r], writes=[aT_r])
                        for fg in range(NFG):
                            iu = base + NOG + 2 * fg
                            prefetch(iu + 4)
                            bu = iu % 4
                            bd = (iu + 1) % 4
                            Wu = wb[bu][:].rearrange("p (k n) -> p k n", k=KC)
                            Wd = wb[bd][:].rearrange("p (f n) -> p f n", f=2)
                            u2 = cnt["u"] % 2
                            cnt["u"] += 1
                            for f in range(2):
                                bank = cnt["ps"] % 2
                                cnt["ps"] += 1
                                for k in range(KC):
                                    op("pe", lambda e: e.matmul(ps[:, bank, :], Wu[:, k, f * 128:(f + 1) * 128], aT[:, k, :],
                                                                start=(k == 0), stop=(k == KC - 1)),
                                       reads=[aT_r, wb_r[bu][k // 16]], writes=[psr[bank]], inc=(k == KC - 1))
                                op("act", lambda e: e.activation(ur[f][:], ps[:, bank, :], AF.Relu),
                                   reads=[psr[bank]], writes=[ur_r[f]])
                                op("act", lambda e: e.activation(uT[u2][:, f, :], ur[f][:], AF.Square),
                                   reads=[ur_r[f]], writes=[uT_r[u2]])
                            for oc in range(KC):
                                bank = 2 + cnt["pd"] % 4
                                cnt["pd"] += 1
                                for f in range(2):
                                    op("pe", lambda e: e.matmul(ps[:, bank, :], Wd[:, f, oc * 128:(oc + 1) * 128], uT[u2][:, f, :],
                                                                start=(f == 0), stop=(f == 1)),
                                       reads=[uT_r[u2], wb_r[bd][f]], writes=[psr[bank]], inc=(f == 1))
                                if fg == NFG - 1:
                                    ss_flush(2)
                                if fg == 0:
                                    if oc % 2 == 0:
                                        op("dve", lambda e: e.tensor_copy(big[:, oc, :], ps[:, bank, :]),
                                           reads=[psr[bank]], writes=[big_r[oc]])
                                    else:
                                        op("act", lambda e: e.activation(big[:, oc, :], ps[:, bank, :], AF.Copy),
                                           reads=[psr[bank]], writes=[big_r[oc]])
                                elif oc % 2 == 0:
                                    op("dve", lambda e: e.tensor_tensor(big[:, oc, :], ps[:, bank, :], big[:, oc, :], ALU.add),
                                       reads=[psr[bank], big_r[oc]], writes=[big_r[oc]])
                                else:
                                    tb = cnt["tm"] % 2
                                    cnt["tm"] += 1
                                    op("act", lambda e: e.activation(tm[tb][:], ps[:, bank, :], AF.Copy),
                                       reads=[psr[bank]], writes=[tm_r[tb]])
                                    op("pool", lambda e: e.tensor_tensor(big[:, oc, :], tm[tb][:], big[:, oc, :], ALU.add),
                                       reads=[tm_r[tb], big_r[oc]], writes=[big_r[oc]])
                                if fg == NFG - 1:
                                    ss_square(oc)
                        ss_rstd()
                        for oc in range(KC):
                            xb = cnt["xs"] % 3
                            cnt["xs"] += 1
                            dma("sp", xs[xb][:], X1T[oc, :, tok], reads=[x1_r[oc]], writes=[xs_r[xb]])
                            tb = cnt["tm"] % 2
                            cnt["tm"] += 1
                            op("dve", lambda e: e.scalar_tensor_tensor(tm[tb][:], big[:, oc, :], gain(3, l, oc), rstd[:],
                                                                       ALU.mult, ALU.mult),
                               reads=[big_r[oc], gains_r, rstd_r], writes=[tm_r[tb]])
                            op("pool", lambda e: e.tensor_tensor(big[:, oc, :], tm[tb][:], xs[xb][:], ALU.add),
                               reads=[tm_r[tb], xs_r[xb]], writes=[big_r[oc]])
                            if not last:
                                dma("sp", XT[oc, :, tok], big[:, oc, :], reads=[big_r[oc]], writes=[xt_r[oc]])
                        if last:
                            for j in range(T // 128):
                                for hf in range(4):
                                    ob = cnt["orow"] % 2
                                    cnt["orow"] += 1
                                    for k4 in range(2):
                                        for kk in range(4):
                                            oc = hf * 8 + k4 * 4 + kk
                                            op("pe", lambda e: e.transpose(ps[:, 7, kk * 128:(kk + 1) * 128],
                                                                           big[:, oc, j * 128:(j + 1) * 128], idn32),
                                               reads=[big_r[oc], cst_r], writes=[psr[7]], inc=(kk == 3))
                                        if k4 % 2 == 0:
                                            op("act", lambda e: e.activation(orow[ob][:, k4 * 512:(k4 + 1) * 512], ps[:, 7, :], AF.Copy),
                                               reads=[psr[7]], writes=[orow_r[ob]])
                                        else:
                                            op("dve", lambda e: e.tensor_copy(orow[ob][:, k4 * 512:(k4 + 1) * 512], ps[:, 7, :]),
                                               reads=[psr[7]], writes=[orow_r[ob]])
                                    r0 = tt * T + j * 128
                                    dma("sp", y_out[r0:r0 + 128, hf * 1024:(hf + 1) * 1024], orow[ob][:], reads=[orow_r[ob]])
                tk.barrier()
        tk.barrier()
    return nc


def _host_consts(pos, boundary):
    half = 64
    inv = 10000.0 ** (-np.arange(half, dtype=np.float64) / half)
    ang = pos.astype(np.float64)[None, :] * inv[:, None]
    cos = np.concatenate([np.cos(ang), np.cos(ang)], axis=0)
    sin = np.concatenate([-np.sin(ang), np.sin(ang)], axis=0)
    ksc = 128.0 ** -0.5
    rot = np.stack([cos, sin, cos * ksc, sin * ksc]).astype(np.float32)
    cst = np.zeros((128, CO_N), np.float32)
    j = np.arange(128)[:, None].astype(np.float64)
    i = np.arange(128)[None, :].astype(np.float64)
    cst[:, CO_DPOS:CO_DPOS + 128] = np.maximum(i - j, 0)
    cst[:, CO_DNEG:CO_DNEG + 128] = np.maximum(j - i, 0)
    cst[:, CO_IP1:CO_IP1 + 128] = np.broadcast_to(i + 1, (128, 128))
    cst[:, CO_CMI:CO_CMI + 128] = np.broadcast_to(128 - i, (128, 128))
    cst[:, CO_IDN:CO_IDN + 128] = np.eye(128)
    sw = np.zeros((128, 128))
    for m in range(128):
        sw[(m + 64) % 128, m] = 1.0
    cst[:, CO_SWP:CO_SWP + 128] = sw
    cst[:, CO_CM1J] = 127 - np.arange(128)
    cst[:, CO_JCOL] = np.arange(128)
    cst[:, CO_C128] = 128.0
    cst[:, CO_NEGC] = NEG if boundary else 0.0
    keepf = np.ones(32)
    keepb = np.ones(32)
    if boundary:
        keepf[16] = 0.0
        keepb[15] = 0.0
    cst[:, CO_KEEPF:CO_KEEPF + 32] = keepf[None, :]
    cst[:, CO_KEEPB:CO_KEEPB + 32] = keepb[None, :]
    return rot, cst


def _rel_bucket_np(rel):
    nbk = 16
    max_exact = 8
    base = np.where(rel > 0, nbk, 0)
    n = np.abs(rel)
    nf = np.maximum(n, 1).astype(np.float32)
    large = max_exact + (np.log(nf / np.float32(max_exact)) / np.float32(np.log(1024 / max_exact))
                         * np.float32(nbk - max_exact)).astype(np.int32)
    large = np.minimum(large, nbk - 1)
    return base + np.where(n < max_exact, n, large)


def _host_onehot():
    oh = np.zeros((3, 33, 384), np.float32)
    for bi, dd in enumerate((1, 4, 16)):
        for xx in range(384):
            x = xx - 64
            delta = 127 - x
            if abs(delta) > 64:
                oh[bi, 32, xx] = 1.0
            else:
                b = int(_rel_bucket_np(np.array([delta * dd], dtype=np.int32))[0])
                oh[bi, b, xx] = 1.0
    return oh


_NC_CACHE = {}


def kernel(x_prompt, x_sample, rel_bias_table, w_in, ret_log_decay, ret_norm_gain, w_out, w_up, w_down,
           norm_mix_pre, norm_mix_post, norm_mlp_pre, norm_mlp_post, _dbg=None):
    f = lambda a: np.ascontiguousarray(np.asarray(a, dtype=np.float32))
    x_prompt, x_sample = f(x_prompt), f(x_sample)
    if _dbg and _dbg.get("smallw"):
        z = np.zeros((1, 1, 1), np.float32)
        w_in = z if "w_in" in _dbg["smallw"] else w_in
        w_out = z if "w_out" in _dbg["smallw"] else w_out
        w_up = z if "w_up" in _dbg["smallw"] else w_up
        w_down = z if "w_down" in _dbg["smallw"] else w_down
    shared = dict(w_in=f(w_in), w_out=f(w_out), w_up=f(w_up), w_down=f(w_down),
                  rel_bias_table=f(rel_bias_table), ret_log_decay=f(ret_log_decay),
                  ret_norm_gain=f(ret_norm_gain), norm_mix_pre=f(norm_mix_pre), norm_mix_post=f(norm_mix_post),
                  norm_mlp_pre=f(norm_mlp_pre), norm_mlp_post=f(norm_mlp_post), onehot=_host_onehot())
    pos_p = np.concatenate([np.arange(2048), np.arange(2048)])
    pos_s = np.arange(4096)
    rot_p, cst_p = _host_consts(pos_p, True)
    rot_s, cst_s = _host_consts(pos_s, False)
    cores = list(range(8)) if not (_dbg and "cores" in _dbg) else _dbg["cores"]
    in_maps = []
    for c in cores:
        m = dict(shared)
        if c < 4:
            m["x"] = x_prompt[2 * c:2 * c + 2].reshape(NTOK, D)
            m["rot"], m["consts"] = rot_p, cst_p
        else:
            m["x"] = x_sample[c - 4].reshape(NTOK, D)
            m["rot"], m["consts"] = rot_s, cst_s
        in_maps.append(m)
    nc = build(_dbg)
    res = run_bass_kernel_spmd(nc, in_maps, core_ids=list(range(len(cores))))
    if _dbg and _dbg.get("raw"):
        return res
    outs = [r["y"] for r in res.results]
    y_prompt = np.stack([o.reshape(2, 2048, D) for o in outs[:4]]).reshape(8, 2048, D)
    y_sample = np.stack(outs[4:8]).reshape(4, 4096, D)
    return (y_prompt.astype(np.float32), y_sample.astype(np.float32))
```

```python
import os
from contextlib import ExitStack
import numpy as np
import concourse.bass as bass
import concourse.mybir as mybir
from concourse.ap import AP
from concourse.bass_utils import run_bass_kernel_spmd

F32 = mybir.dt.float32
BF16 = mybir.dt.bfloat16
ALU = mybir.AluOpType
AF = mybir.ActivationFunctionType

D = 4096
NTOK = 4096
T = 512
NT = NTOK // T
KC = 32
DEPTH = 2
INW = 12288
DFF = 16384
EPS = 1e-6
NEG = -30000.0
C = 128
NCH = NTOK // C

CO_DPOS = 0
CO_DNEG = 128
CO_IP1 = 256
CO_CMI = 384
CO_IDN = 512
CO_SWP = 640
CO_CM1J = 768
CO_JCOL = 769
CO_C128 = 770
CO_NEGC = 771
CO_KEEPF = 772
CO_KEEPB = 804
CO_N = 836


class Res:
    __slots__ = ("w", "re", "rd", "excl")

    def __init__(self, excl=False):
        self.w = None
        self.re = {}
        self.rd = []
        self.excl = excl

    def add_read(self, tok):
        if tok[0] == "e":
            if self.re.get(tok[1], 0) < tok[2]:
                self.re[tok[1]] = tok[2]
        else:
            self.rd.append(tok)

    def set_write(self, tok):
        self.w = tok
        self.re = {}
        self.rd = []


class Tk:
    def __init__(self, nc, nslots=22):
        self.nc = nc
        self.E = {}
        for name, eng in (("pe", nc.tensor), ("dve", nc.vector), ("act", nc.scalar),
                          ("pool", nc.gpsimd), ("sp", nc.sync)):
            self.E[name] = dict(eng=eng, sem=nc.alloc_semaphore("s_" + name), count=0, seen={}, seen_d={})
        self.dq = {}
        for q in ("sp", "pool"):
            self.dq[q] = dict(slots=[dict(sem=nc.alloc_semaphore(f"d_{q}{i}"), val=0, id=(q, i))
                                     for i in range(nslots)], next=0)

    def _wait(self, en, dep):
        e = self.E[en]
        if dep[0] == "e":
            _, name, c = dep
            if name == en and en in ("pe", "sp"):
                return
            if e["seen"].get(name, 0) >= c:
                return
            e["eng"].wait_ge(self.E[name]["sem"], c)
            e["seen"][name] = c
        else:
            _, slot, v = dep
            if e["seen_d"].get(slot["id"], 0) >= v:
                return
            e["eng"].wait_ge(slot["sem"], v)
            e["seen_d"][slot["id"]] = v

    def _waitall(self, en, reads, writes):
        for r in reads:
            if r.w is not None:
                self._wait(en, r.w)
            if r.excl:
                for n, c in list(r.re.items()):
                    if n != en:
                        self._wait(en, ("e", n, c))
        for w in writes:
            if w.w is not None:
                self._wait(en, w.w)
            for n, c in list(w.re.items()):
                self._wait(en, ("e", n, c))
            for d in w.rd:
                self._wait(en, d)

    def op(self, en, fn, reads=(), writes=(), inc=True):
        e = self.E[en]
        self._waitall(en, reads, writes)
        ins = fn(e["eng"])
        tok = ("e", en, e["count"] + 1)
        if inc:
            ins.then_inc(e["sem"], 1)
            e["count"] += 1
        for r in reads:
            r.add_read(tok)
        for w in writes:
            w.set_write(tok)
        return ins

    def dma(self, q, out, in_, reads=(), writes=(), **kw):
        e = self.E[q]
        dq = self.dq[q]
        slot = dq["slots"][dq["next"] % len(dq["slots"])]
        dq["next"] += 1
        self._waitall(q, reads, writes)
        if slot["val"] > 0:
            self._wait(q, ("d", slot, slot["val"]))
        ins = e["eng"].dma_start(out=out, in_=in_, **kw)
        slot["val"] += 16
        ins.then_inc(slot["sem"], 16)
        tok = ("d", slot, slot["val"])
        for r in reads:
            r.add_read(tok)
        for w in writes:
            w.set_write(tok)
        return ins

    def barrier(self):
        for en, e in self.E.items():
            for on, oe in self.E.items():
                if on != en and oe["count"] > 0:
                    self._wait(en, ("e", on, oe["count"]))
            for q in self.dq.values():
                for slot in q["slots"]:
                    if slot["val"] > 0:
                        self._wait(en, ("d", slot, slot["val"]))


def _col_kind(g):
    c0 = g * 256
    if c0 < 1024:
        return "rq"
    if c0 < 2048:
        return "rk"
    if c0 < 4096:
        return "rv"
    if c0 < 6144:
        return "rg"
    if c0 < 8192:
        return "aq"
    if c0 < 10240:
        return "ak"
    return "av"


def build(dbg=None):
    dbg = dbg or {}
    phases = dbg.get("phases", None)
    dump = dbg.get("dump", False)
    nlayers = dbg.get("layers", DEPTH)

    def want(p):
        return phases is None or p in phases

    nc = bass.Bass("TRN2", target_bir_lowering=False)
    skind = "ExternalOutput" if dump else "Internal"

    def din(name, shape, dt=F32):
        return nc.dram_tensor(name, list(shape), dt, kind="ExternalInput").ap()

    x_in = din("x", [NTOK, D])
    smallw = dbg.get("smallw", ())
    w_in = din("w_in", [DEPTH, D, INW] if "w_in" not in smallw else [1, 1, 1])
    w_out = din("w_out", [DEPTH, D, D] if "w_out" not in smallw else [1, 1, 1])
    w_up = din("w_up", [DEPTH, D, DFF] if "w_up" not in smallw else [1, 1, 1])
    w_down = din("w_down", [DEPTH, DFF, D] if "w_down" not in smallw else [1, 1, 1])
    relb = din("rel_bias_table", [32, 16])
    rld = din("ret_log_decay", [DEPTH, 2, 8])
    rng = din("ret_norm_gain", [DEPTH, 8, 256])
    g_mix_pre = din("norm_mix_pre", [DEPTH, D])
    g_mix_post = din("norm_mix_post", [DEPTH, D])
    g_mlp_pre = din("norm_mlp_pre", [DEPTH, D])
    g_mlp_post = din("norm_mlp_post", [DEPTH, D])
    rot = din("rot", [4, 128, NTOK])
    consts = din("consts", [128, CO_N])
    onehot = din("onehot", [3, 33, 384])
    y_out = nc.dram_tensor("y", [NTOK, D], F32, kind="ExternalOutput").ap()

    XT_t = nc.dram_tensor("XT", [KC, 128, NTOK], F32, kind=skind)
    X1T_t = nc.dram_tensor("X1T", [KC, 128, NTOK], F32, kind=skind)
    FT_t = nc.dram_tensor("FT", [64, 128, NTOK], BF16, kind=skind)
    VT_t = nc.dram_tensor("VT", [NTOK, 4096], BF16, kind=skind)
    YT_t = nc.dram_tensor("YT", [KC, 128, NTOK], BF16, kind=skind)
    U_t = nc.dram_tensor("U", [3, 16, 129, 384], F32, kind=skind)
    XT, X1T, FT, VT, YT, U = (t.ap() for t in (XT_t, X1T_t, FT_t, VT_t, YT_t, U_t))

    tk = Tk(nc)
    op = tk.op
    dma = tk.dma

    with ExitStack() as top:
        uid = [0]

        def sb(name, shape, dt, es=top):
            uid[0] += 1
            return es.enter_context(nc.sbuf_tensor(f"{name}_{uid[0]}", list(shape), dt))

        ps = top.enter_context(nc.psum_tensor("ps", [128, 8, 512], F32))
        psr = [Res(excl=True) for _ in range(8)]

        cst = sb("cst", [128, CO_N], F32)
        cst_r = Res()
        dma("sp", cst[:], consts[:, :], writes=[cst_r])
        idn_bf = sb("idn_bf", [128, 128], BF16)
        swp_bf = sb("swp_bf", [128, 128], BF16)
        ones_bf = sb("ones_bf", [128, 128], BF16)
        om4096 = sb("om4096", [128, 128], BF16)
        om256 = sb("om256", [128, 128], BF16)
        epsc = sb("epsc", [128, 1], F32)
        k_r = Res()
        op("dve", lambda e: e.tensor_copy(idn_bf[:], cst[:, CO_IDN:CO_IDN + 128]), reads=[cst_r], writes=[k_r])
        op("dve", lambda e: e.tensor_copy(swp_bf[:], cst[:, CO_SWP:CO_SWP + 128]), reads=[cst_r], writes=[k_r])
        op("dve", lambda e: e.memset(ones_bf[:], 1.0), writes=[k_r])
        op("dve", lambda e: e.memset(om4096[:], 1.0 / 4096.0), writes=[k_r])
        op("dve", lambda e: e.memset(om256[:], 1.0 / 256.0), writes=[k_r])
        op("dve", lambda e: e.memset(epsc[:], EPS), writes=[k_r])
        idn32 = cst[:, CO_IDN:CO_IDN + 128]
        gains = sb("gains", [128, 4 * DEPTH, KC], F32)
        rgn = sb("rgn", [128, DEPTH * 8 * 2], F32)
        gains_r = Res()
        with ExitStack() as es0:
            gst = sb("gst", [128, 3, 128], F32, es0)
            gst_r = Res()
            op("dve", lambda e: e.memset(gst[:], 0.0), writes=[gst_r])
            for gi, gsrc in enumerate((g_mix_pre, g_mix_post, g_mlp_pre, g_mlp_post)):
                for l in range(DEPTH):
                    v = gi * DEPTH + l
                    dma("sp", gst[(v % 4) * 32:(v % 4 + 1) * 32, v // 4, :], gsrc[l].rearrange("(k p) -> k p", p=128),
                        writes=[gst_r])
            dma("sp", gst[0:32, 2, :], rng.rearrange("l h (e p) -> (l h e) p", p=128), writes=[gst_r])
            for i3 in range(3):
                op("pe", lambda e: e.transpose(ps[:, 0, i3 * 128:(i3 + 1) * 128], gst[:, i3, :], cst[:, CO_IDN:CO_IDN + 128]),
                   reads=[gst_r, cst_r], writes=[psr[0]])
            op("dve", lambda e: e.tensor_copy(gains[:].rearrange("p a k -> p (a k)"), ps[:, 0, 0:256]),
               reads=[psr[0]], writes=[gains_r])
            op("dve", lambda e: e.tensor_copy(rgn[:], ps[:, 0, 256:288]), reads=[psr[0]], writes=[gains_r])
            tk.barrier()

        def gain(gi, l, k):
            return gains[:, gi * DEPTH + l, k:k + 1]

        lam = sb("lam", [128, DEPTH * 16], F32)
        lam_r = Res()
        dma("sp", lam[:], rld.rearrange("l d h -> (l d h)").partition_broadcast(128), writes=[lam_r])
        op("act", lambda e: e.activation(lam[:], lam[:], AF.Exp), reads=[lam_r], writes=[lam_r])
        op("dve", lambda e: e.tensor_scalar(lam[:], lam[:], -1.0, None, ALU.mult), reads=[lam_r], writes=[lam_r])

        if want("bias"):
            with ExitStack() as es:
                tab = sb("tab", [33, 16], F32, es)
                oh = sb("oh", [33, 3, 384], F32, es)
                ust = sb("ust", [16, 3, 384], F32, es)
                r1 = Res()
                op("dve", lambda e: e.memset(tab[32:33, :], NEG), writes=[r1])
                dma("sp", tab[0:32, :], relb[:, :], writes=[r1])
                dma("sp", oh[:], onehot.rearrange("d b x -> b d x"), writes=[r1])
                for d3 in range(3):
                    op("pe", lambda e: e.matmul(ps[0:16, d3, 0:384], tab[:, :], oh[:, d3, :], start=True, stop=True),
                       reads=[r1], writes=[psr[d3]])
                    op("dve", lambda e: e.tensor_copy(ust[:, d3, :], ps[0:16, d3, 0:384]), reads=[psr[d3]], writes=[r1])
                for d3 in range(3):
                    dma("sp", U[d3], ust[:, d3, :].unsqueeze(1).to_broadcast([16, 129, 384]), reads=[r1])
            tk.barrier()

        if want("xt"):
            with ExitStack() as es:
                xin = [sb(f"xin{i}", [128, D], F32, es) for i in range(2)]
                xin_r = [Res() for _ in range(2)]
                xst = [sb(f"xst{i}", [128, KC, 128], F32, es) for i in range(2)]
                xst_r = [Res() for _ in range(2)]
                for s in range(NTOK // 128):
                    b = s % 2
                    dma("sp", xin[b][:], x_in[s * 128:(s + 1) * 128, :], writes=[xin_r[b]])
                    for k4 in range(8):
                        bank = (s * 8 + k4) % 4
                        for kk in range(4):
                            k = k4 * 4 + kk
                            op("pe", lambda e: e.transpose(ps[:, bank, kk * 128:(kk + 1) * 128],
                                                           xin[b][:, k * 128:(k + 1) * 128], idn32),
                               reads=[xin_r[b], cst_r], writes=[psr[bank]], inc=(kk == 3))
                        eng = "dve" if k4 % 2 == 0 else "act"
                        if eng == "dve":
                            op("dve", lambda e: e.tensor_copy(xst[b][:, k4 * 4:(k4 + 1) * 4, :],
                                                              ps[:, bank, :].rearrange("p (a b) -> p a b", a=4)),
                               reads=[psr[bank]], writes=[xst_r[b]])
                        else:
                            op("act", lambda e: e.activation(xst[b][:, k4 * 4:(k4 + 1) * 4, :],
                                                             ps[:, bank, :].rearrange("p (a b) -> p a b", a=4), AF.Copy),
                               reads=[psr[bank]], writes=[xst_r[b]])
                    for hh in range(2):
                        dma("sp", XT[hh * 16:(hh + 1) * 16, :, s * 128:(s + 1) * 128].rearrange("k p t -> p k t"),
                            xst[b][:, hh * 16:(hh + 1) * 16, :], reads=[xst_r[b]])
            tk.barrier()

        for l in range(nlayers):
            last = (l == DEPTH - 1)
            if want("A"):
                with ExitStack() as es:
                    big = sb("big", [128, KC, T], F32, es)
                    big_r = [Res() for _ in range(KC)]
                    hT = sb("hT", [128, KC, T], BF16, es)
                    hT_r = Res()
                    wb = [sb(f"wb{i}", [128, KC, 256], BF16, es) for i in range(4)]
                    wb_r = [[Res(), Res()] for _ in range(4)]
                    rt = sb("rt", [128, 4, T], F32, es)
                    rt_r = Res()
                    sq = [sb(f"sq{i}", [128, T], BF16, es) for i in range(2)]
                    sq_r = [Res() for _ in range(2)]
                    rstd = sb("rstd", [128, T], F32, es)
                    rstd_r = Res()
                    ost = [sb(f"ost{i}", [128, T], BF16, es) for i in range(4)]
                    ost_r = [Res() for _ in range(4)]
                    qb = [sb(f"qb{i}", [128, T], BF16, es) for i in range(2)]
                    qb_r = [Res() for _ in range(2)]
                    t1 = [sb(f"t1{i}", [128, T], F32, es) for i in range(2)]
                    t1_r = [Res() for _ in range(2)]
                    t2 = [sb(f"t2{i}", [128, T], F32, es) for i in range(2)]
                    t2_r = [Res() for _ in range(2)]
                    NG = INW // 256
                    a_groups = dbg.get("A_groups", None)
                    glist = list(range(NG)) if a_groups is None else list(a_groups)
                    NG = len(glist)
                    NTA = dbg.get("A_tiles", NT)

                    def issue_w(idx):
                        tt, gi_ = divmod(idx, NG)
                        g = glist[gi_]
                        b = idx % 4
                        for hh in range(2):
                            dma("pool", wb[b][:, hh * 16:(hh + 1) * 16, :],
                                w_in[l, hh * 2048:(hh + 1) * 2048, g * 256:(g + 1) * 256].rearrange("(k p) n -> p k n", p=128),
                                writes=[wb_r[b][hh]])

                    total = NTA * NG
                    nissued = 0
                    cnt = dict(ps=0, ost=0, rot=0, sq=0)
                    for tt in range(NTA):
                        tok = slice(tt * T, (tt + 1) * T)
                        while nissued < min(total, tt * NG + 3):
                            issue_w(nissued)
                            nissued += 1
                        for q4 in range(4):
                            dma("sp", big[:, q4 * 8:(q4 + 1) * 8, :],
                                XT[q4 * 8:(q4 + 1) * 8, :, tok].rearrange("k p t -> p k t"),
                                writes=big_r[q4 * 8:(q4 + 1) * 8])
                        dma("sp", rt[:], rot[:, :, tok].rearrange("f p t -> p f t"), writes=[rt_r])
                        for k in range(KC):
                            b = cnt["sq"] % 2
                            cnt["sq"] += 1
                            op("act", lambda e: e.activation(sq[b][:], big[:, k, :], AF.Square),
                               reads=[big_r[k]], writes=[sq_r[b]])
                            op("pe", lambda e: e.matmul(ps[:, 6, :], om4096[:], sq[b][:], start=(k == 0), stop=(k == KC - 1)),
                               reads=[sq_r[b], k_r], writes=[psr[6]])
                        op("act", lambda e: e.activation(rstd[:], ps[:, 6, :], AF.Sqrt, bias=epsc[:, 0:1]),
                           reads=[psr[6], k_r], writes=[rstd_r])
                        op("dve", lambda e: e.reciprocal(rstd[:], rstd[:]), reads=[rstd_r], writes=[rstd_r])
                        for k in range(KC):
                            en = "dve"
                            op(en, lambda e: e.scalar_tensor_tensor(hT[:, k, :], big[:, k, :], gain(0, l, k), rstd[:],
                                                                    ALU.mult, ALU.mult),
                               reads=[big_r[k], rstd_r, gains_r], writes=[hT_r])
                        for gi_ in range(NG):
                            g = glist[gi_]
                            idx = tt * NG + gi_
                            while nissued < min(total, idx + 3):
                                issue_w(nissued)
                                nissued += 1
                            b = idx % 4
                            kind = _col_kind(g)
                            W = wb[b]
                            amode = dbg.get("A_mode", 3)
                            if amode == 0:
                                op("dve", lambda e: e.tensor_copy(ost[0][:, 0:256], W[:, 3, :]), reads=wb_r[b], writes=[ost_r[0]])
                                continue
                            for c2 in range(2):
                                col0 = g * 256 + c2 * 128
                                bank = cnt["ps"] % 4
                                cnt["ps"] += 1
                                if kind in ("rv", "av"):
                                    for j2 in range(2):
                                        j = c2 * 2 + j2
                                        for k in range(KC):
                                            op("pe", lambda e: e.matmul(ps[:, bank, j2 * 256:(j2 + 1) * 256],
                                                                        hT[:, k, j * 128:(j + 1) * 128], W[:, k, :],
                                                                        start=(k == 0), stop=(k == KC - 1)),
                                               reads=[hT_r, wb_r[b][k // 16]], writes=[psr[bank]], inc=(k == KC - 1))
                                    o = cnt["ost"] % 4
                                    cnt["ost"] += 1
                                    en = "act" if cnt["ost"] % 2 == 0 else "dve"
                                    if en == "act":
                                        op("act", lambda e: e.activation(ost[o][:], ps[:, bank, :], AF.Copy),
                                           reads=[psr[bank]], writes=[ost_r[o]])
                                    else:
                                        op("dve", lambda e: e.tensor_copy(ost[o][:], ps[:, bank, :]),
                                           reads=[psr[bank]], writes=[ost_r[o]])
                                    vc0 = (g * 256 - 2048) if kind == "rv" else (g * 256 - 10240 + 2048)
                                    r0 = tt * T + c2 * 256
                                    dma("sp", VT[r0:r0 + 256, vc0:vc0 + 256].rearrange("(j p) c -> p j c", p=128),
                                        ost[o][:].rearrange("p (j c) -> p j c", j=2), reads=[ost_r[o]])
                                    continue
                                for k in range(KC):
                                    op("pe", lambda e: e.matmul(ps[:, bank, :], W[:, k, c2 * 128:(c2 + 1) * 128], hT[:, k, :],
                                                                start=(k == 0), stop=(k == KC - 1)),
                                       reads=[hT_r, wb_r[b][k // 16]], writes=[psr[bank]], inc=(k == KC - 1))
                                if amode == 1:
                                    continue
                                o = cnt["ost"] % 4
                                cnt["ost"] += 1
                                if kind in ("rq", "rk"):
                                    fi = 0 if kind == "rq" else 2
                                    ch = col0 // 128
                                    r = cnt["rot"] % 2
                                    cnt["rot"] += 1
                                    op("act", lambda e: e.activation(qb[r][:], ps[:, bank, :], AF.Copy),
                                       reads=[psr[bank]], writes=[qb_r[r]])
                                    op("dve", lambda e: e.tensor_tensor(t1[r][:], ps[:, bank, :], rt[:, fi, :], ALU.mult),
                                       reads=[psr[bank], rt_r, qb_r[r]], writes=[t1_r[r]])
                                    if amode == 4:
                                        continue
                                    op("pe", lambda e: e.matmul(ps[:, 4 + r, :], swp_bf[:], qb[r][:], start=True, stop=True),
                                       reads=[qb_r[r], k_r], writes=[psr[4 + r]])
                                    op("dve", lambda e: e.tensor_tensor(t2[r][:], ps[:, 4 + r, :], rt[:, fi + 1, :], ALU.mult),
                                       reads=[psr[4 + r], rt_r], writes=[t2_r[r]])
                                    if amode == 5:
                                        continue
                                    op("dve", lambda e: e.tensor_tensor(ost[o][:], t1[r][:], t2[r][:], ALU.add),
                                       reads=[t1_r[r], t2_r[r]], writes=[ost_r[o]])
                                    if amode == 6:
                                        continue
                                elif kind == "rg":
                                    ch = 48 + (col0 - 4096) // 128
                                    op("act", lambda e: e.activation(ost[o][:], ps[:, bank, :], AF.Silu),
                                       reads=[psr[bank]], writes=[ost_r[o]])
                                elif kind == "aq":
                                    ch = 16 + (col0 - 6144) // 128
                                    op("act", lambda e: e.mul(ost[o][:], ps[:, bank, :], float(128 ** -0.5)),
                                       reads=[psr[bank]], writes=[ost_r[o]])
                                else:
                                    ch = 32 + (col0 - 8192) // 128
                                    op("dve", lambda e: e.tensor_copy(ost[o][:], ps[:, bank, :]),
                                       reads=[psr[bank]], writes=[ost_r[o]])
                                dma("sp", FT[ch, :, tok], ost[o][:], reads=[ost_r[o]])
                tk.barrier()

            if want("B1"):
                with ExitStack() as es:
                    QT = sb("rQT", [128, NTOK], BF16, es)
                    KT = sb("rKT", [128, NTOK], BF16, es)
                    GT = sb("rGT", [128, 2, NTOK], BF16, es)
                    Vr = sb("rV", [128, NCH, 256], BF16, es)
                    Yr = sb("rY", [128, 2, NTOK], BF16, es)
                    QfA = sb("rQfA", [128, NTOK], BF16, es)
                    QbA = sb("rQbA", [128, NTOK], BF16, es)
                    Kf_all = sb("rKf", [128, NCH, 128], BF16, es)
                    Kb_all = sb("rKb", [128, NCH, 128], BF16, es)
                    Sb_all = sb("rSb", [128, NCH, 256], BF16, es)
                    q_r, kk_r, g_r, v_r = Res(), Res(), Res(), Res()
                    Yr_r, Kf_r, Kb_r, Sb_r, QfA_r, QbA_r = Res(), Res(), Res(), Res(), Res(), Res()
                    maskT = sb("maskT", [128, 128], F32, es)
                    mtmp = sb("mtmp", [128, 128], F32, es)
                    qdf = sb("qdf", [128, 128], F32, es)
                    qdb = sb("qdb", [128, 128], F32, es)
                    hc = sb("hc", [128, 8], F32, es)
                    kdk = sb("kdk", [128, 4, NCH], F32, es)
                    hc_r = Res()
                    S32 = sb("S32", [128, 256], F32, es)
                    S_r = Res()
                    Sf_bf = [sb(f"Sfbf{i}", [128, 256], BF16, es) for i in range(2)]
                    Sf_r = [Res() for _ in range(2)]
                    St = [sb(f"St{i}", [128, 128], BF16, es) for i in range(2)]
                    St_r = [Res() for _ in range(2)]
                    ysq = [sb(f"ysq{i}", [128, 2, 128], BF16, es) for i in range(2)]
                    ysq_r = [Res() for _ in range(2)]
                    yrs = [sb(f"yrs{i}", [128, 128], F32, es) for i in range(2)]
                    yrs_r = [Res() for _ in range(2)]
                    yt = [sb(f"yt{i}", [128, 2, 128], F32, es) for i in range(2)]
                    yt_r = [Res() for _ in range(2)]
                    keepf_sh = cst[:, CO_KEEPF + 1:CO_KEEPF + NCH]
                    keepb_sh = cst[:, CO_KEEPB:CO_KEEPB + NCH - 1]
                    for h in range(8):
                        lf = lam[:, l * 16 + h:l * 16 + h + 1]
                        lb = lam[:, l * 16 + 8 + h:l * 16 + 8 + h + 1]
                        dma("sp", QT[:], FT[h, :, :], writes=[q_r])
                        dma("sp", KT[:], FT[8 + h, :, :], writes=[kk_r])
                        dma("sp", GT[:], FT[48 + 2 * h:48 + 2 * h + 2, :, :].rearrange("e p t -> p e t"), writes=[g_r])
                        for hh in range(2):
                            dma("sp", Vr[:, hh * 16:(hh + 1) * 16, :],
                                VT[hh * 2048:(hh + 1) * 2048, h * 256:(h + 1) * 256].rearrange("(n p) c -> p n c", p=128), writes=[v_r])
                        op("dve", lambda e: e.tensor_scalar(mtmp[:], cst[:, CO_DPOS:CO_DPOS + 128], lf, None, ALU.mult),
                           reads=[cst_r, lam_r], writes=[hc_r])
                        op("dve", lambda e: e.scalar_tensor_tensor(mtmp[:], cst[:, CO_DNEG:CO_DNEG + 128], lb, mtmp[:],
                                                                   ALU.mult, ALU.add),
                           reads=[cst_r, lam_r, hc_r], writes=[hc_r])
                        op("act", lambda e: e.activation(maskT[:], mtmp[:], AF.Exp), reads=[hc_r], writes=[hc_r])
                        op("act", lambda e: e.activation(qdf[:], cst[:, CO_IP1:CO_IP1 + 128], AF.Exp, scale=lf),
                           reads=[cst_r, lam_r], writes=[hc_r])
                        op("act", lambda e: e.activation(qdb[:], cst[:, CO_CMI:CO_CMI + 128], AF.Exp, scale=lb),
                           reads=[cst_r, lam_r], writes=[hc_r])
                        op("act", lambda e: e.activation(hc[:, 0:1], cst[:, CO_CM1J:CO_CM1J + 1], AF.Exp, scale=lf),
                           reads=[cst_r, lam_r], writes=[hc_r])
                        op("act", lambda e: e.activation(hc[:, 1:2], cst[:, CO_JCOL:CO_JCOL + 1], AF.Exp, scale=lb),
                           reads=[cst_r, lam_r], writes=[hc_r])
                        op("act", lambda e: e.activation(hc[:, 2:3], cst[:, CO_C128:CO_C128 + 1], AF.Exp, scale=lf),
                           reads=[cst_r, lam_r], writes=[hc_r])
                        op("act", lambda e: e.activation(hc[:, 3:4], cst[:, CO_C128:CO_C128 + 1], AF.Exp, scale=lb),
                           reads=[cst_r, lam_r], writes=[hc_r])
                        op("dve", lambda e: e.memset(kdk[:], 0.0), reads=[hc_r], writes=[hc_r])
                        op("dve", lambda e: e.tensor_scalar(kdk[:, 0, 0:NCH - 1], keepf_sh, hc[:, 0:1], None, ALU.mult),
                           reads=[hc_r, cst_r], writes=[hc_r])
                        op("dve", lambda e: e.tensor_scalar(kdk[:, 1, 1:NCH], keepb_sh, hc[:, 1:2], None, ALU.mult),
                           reads=[hc_r, cst_r], writes=[hc_r])
                        op("dve", lambda e: e.tensor_scalar(kdk[:, 2, 0:NCH - 1], keepf_sh, hc[:, 2:3], None, ALU.mult),
                           reads=[hc_r, cst_r], writes=[hc_r])
                        op("dve", lambda e: e.tensor_scalar(kdk[:, 3, 1:NCH], keepb_sh, hc[:, 3:4], None, ALU.mult),
                           reads=[hc_r, cst_r], writes=[hc_r])
                        op("dve", lambda e: e.tensor_tensor(QfA[:].rearrange("p (n i) -> p n i", i=C), QT[:].rearrange("p (n i) -> p n i", i=C),
                                                             qdf[:].unsqueeze(1).to_broadcast([128, NCH, C]), ALU.mult),
                           reads=[q_r, hc_r], writes=[QfA_r])
                        op("dve", lambda e: e.tensor_tensor(QbA[:].rearrange("p (n i) -> p n i", i=C), QT[:].rearrange("p (n i) -> p n i", i=C),
                                                             qdb[:].unsqueeze(1).to_broadcast([128, NCH, C]), ALU.mult),
                           reads=[q_r, hc_r], writes=[QbA_r])
                        for e2 in range(2):
                            gcol = rgn[:, (l * 8 + h) * 2 + e2:(l * 8 + h) * 2 + e2 + 1]
                            op("act", lambda e: e.mul(GT[:, e2, :], GT[:, e2, :], gcol),
                               reads=[g_r, gains_r], writes=[g_r])
                        for n in range(NCH):
                            cs = slice(n * C, (n + 1) * C)
                            i2 = n % 2
                            pT = ps[:, 0 + i2, 0:64].bitcast(BF16)
                            op("pe", lambda e: e.transpose(pT, KT[:, cs], idn_bf[:]),
                               reads=[kk_r, k_r], writes=[psr[0 + i2]])
                            if n < NCH - 1:
                                op("act", lambda e: e.mul(Kf_all[:, n, :], pT, kdk[:, 0, n:n + 1]),
                                   reads=[psr[0 + i2], hc_r], writes=[Kf_r])
                            if n > 0:
                                op("act", lambda e: e.mul(Kb_all[:, n, :], pT, kdk[:, 1, n:n + 1]),
                                   reads=[psr[0 + i2], hc_r], writes=[Kb_r])
                        op("dve", lambda e: e.memset(S32[:], 0.0), writes=[S_r])
                        op("pool", lambda e: e.memset(Sb_all[:, NCH - 1, :], 0.0), writes=[Sb_r])
                        for n in range(NCH - 1, 0, -1):
                            i2 = n % 2
                            op("pe", lambda e: e.matmul(ps[:, 2 + i2, 0:256], Kb_all[:, n, :], Vr[:, n, :], start=True, stop=True),
                               reads=[Kb_r, v_r], writes=[psr[2 + i2]])
                            op("dve", lambda e: e.scalar_tensor_tensor(S32[:], S32[:], kdk[:, 3, n:n + 1], ps[:, 2 + i2, 0:256],
                                                                       ALU.mult, ALU.add),
                               reads=[S_r, hc_r, psr[2 + i2]], writes=[S_r])
                            op("act", lambda e: e.activation(Sb_all[:, n - 1, :], S32[:], AF.Copy), reads=[S_r], writes=[Sb_r])
                        op("dve", lambda e: e.memset(S32[:], 0.0), writes=[S_r])
                        op("pool", lambda e: e.memset(Sf_bf[0][:], 0.0), writes=[Sf_r[0]])

                        def stA(n):
                            cs = slice(n * C, (n + 1) * C)
                            i2 = n % 2
                            op("pe", lambda e: e.matmul(ps[:, i2, 0:128], KT[:, cs], QT[:, cs], start=True, stop=True),
                               reads=[kk_r, q_r], writes=[psr[i2]])
                            op("dve", lambda e: e.tensor_tensor(St[i2][:], ps[:, i2, 0:128], maskT[:], ALU.mult),
                               reads=[psr[i2], hc_r], writes=[St_r[i2]])

                        def stB(n):
                            cs = slice(n * C, (n + 1) * C)
                            i2 = n % 2
                            yb = 5 if i2 == 0 else 7
                            for e2 in range(2):
                                es_ = slice(e2 * 128, (e2 + 1) * 128)
                                po = ps[:, yb, e2 * 128:(e2 + 1) * 128]
                                op("pe", lambda e: e.matmul(po, Vr[:, n, es_], St[i2][:], start=True, stop=False),
                                   reads=[v_r, St_r[i2]], writes=[psr[yb]], inc=False)
                                op("pe", lambda e: e.matmul(po, Sf_bf[i2][:, es_], QfA[:, cs], start=False, stop=False),
                                   reads=[Sf_r[i2], QfA_r], writes=[psr[yb]], inc=False)
                                op("pe", lambda e: e.matmul(po, Sb_all[:, n, es_], QbA[:, cs], start=False, stop=True),
                                   reads=[Sb_r, QbA_r], writes=[psr[yb]], inc=(e2 == 1))
                            if n < NCH - 1:
                                op("pe", lambda e: e.matmul(ps[:, 2 + i2, 0:256], Kf_all[:, n, :], Vr[:, n, :], start=True, stop=True),
                                   reads=[Kf_r, v_r], writes=[psr[2 + i2]])
                                op("dve", lambda e: e.scalar_tensor_tensor(S32[:], S32[:], kdk[:, 2, n:n + 1], ps[:, 2 + i2, 0:256],
                                                                           ALU.mult, ALU.add),
                                   reads=[S_r, hc_r, psr[2 + i2]], writes=[S_r])
                                op("act", lambda e: e.activation(Sf_bf[1 - i2][:], S32[:], AF.Copy), reads=[S_r], writes=[Sf_r[1 - i2]])
                            op("act", lambda e: e.activation(ysq[i2][:], ps[:, yb, 0:256].rearrange("p (a b) -> p a b", a=2), AF.Square),
                               reads=[psr[yb]], writes=[ysq_r[i2]])

                        def stC(n):
                            cs = slice(n * C, (n + 1) * C)
                            i2 = n % 2
                            yb = 5 if i2 == 0 else 7
                            for e2 in range(2):
                                op("pe", lambda e: e.matmul(ps[:, 6, 0:128], om256[:], ysq[i2][:, e2, :], start=(e2 == 0), stop=(e2 == 1)),
                                   reads=[ysq_r[i2], k_r], writes=[psr[6]], inc=(e2 == 1))
                            op("act", lambda e: e.activation(yrs[i2][:], ps[:, 6, 0:128], AF.Sqrt, bias=epsc[:, 0:1]),
                               reads=[psr[6], k_r], writes=[yrs_r[i2]])
                            op("dve", lambda e: e.reciprocal(yrs[i2][:], yrs[i2][:]), reads=[yrs_r[i2]], writes=[yrs_r[i2]])
                            op("dve", lambda e: e.tensor_tensor(yt[i2][:], ps[:, yb, 0:256].rearrange("p (a b) -> p a b", a=2),
                                                                yrs[i2][:].unsqueeze(1).to_broadcast([128, 2, 128]), ALU.mult),
                               reads=[psr[yb], yrs_r[i2]], writes=[yt_r[i2]])
                            op("pool", lambda e: e.tensor_tensor(Yr[:, :, cs], yt[i2][:], GT[:, :, cs], ALU.mult),
                               reads=[yt_r[i2], g_r], writes=[Yr_r])

                        for n in range(NCH + 2):
                            if n < NCH:
                                stA(n)
                            if 0 <= n - 1 < NCH:
                                stB(n - 1)
                            if 0 <= n - 2 < NCH:
                                stC(n - 2)
                        dma("sp", YT[2 * h:2 * h + 2, :, :].rearrange("e p t -> p e t"), Yr[:], reads=[Yr_r])
                tk.barrier()

            if want("B2"):
                with ExitStack() as es:
                    QT = sb("aQT", [128, NTOK], BF16, es)
                    KT = sb("aKT", [128, NTOK], BF16, es)
                    Vd = [sb(f"aV{i}", [128, 32, 128], BF16, es) for i in range(3)]
                    acc = sb("aacc", [128, 2, NTOK], F32, es)
                    yo = sb("ayo", [128, NTOK], BF16, es)
                    Bx = sb("Bx", [128, 3, 3, 256], F32, es)
                    q_r, kk_r, v_r = Res(), Res(), Res()
                    b_r = Res()
                    acc_r = [Res() for _ in range(32)]
                    yo_r = Res()
                    NB = 3
                    scs = [sb(f"scs{i}", [128, 256], F32, es) for i in range(NB)]
                    scs_r = [Res() for _ in range(NB)]
                    Pm = [sb(f"Pm{i}", [128, 256], BF16, es) for i in range(NB)]
                    Pm_r = [Res() for _ in range(NB)]
                    negc = cst[:, CO_NEGC:CO_NEGC + 1]
                    for h in range(16):
                        dma("sp", QT[:], FT[16 + h, :, :], writes=[q_r])
                        dma("sp", KT[:], FT[32 + h, :, :], writes=[kk_r])
                        vsrc = VT[:, 2048 + h * 128:2048 + (h + 1) * 128]
                        for hh in range(2):
                            dma("sp", Vd[0][:, hh * 16:(hh + 1) * 16, :],
                                VT[hh * 2048:(hh + 1) * 2048, 2048 + h * 128:2048 + (h + 1) * 128].rearrange("(m p) c -> p m c", p=128),
                                writes=[v_r])
                        for bi, dd in ((1, 4), (2, 16)):
                            v4 = vsrc.rearrange("(m p g) c -> g p m c", p=128, g=dd)
                            mpg = 32 // dd
                            for g in range(dd):
                                dma("sp", Vd[bi][:, g * mpg:(g + 1) * mpg, :], v4[g], writes=[v_r])
                        for bi in range(3):
                            base = (bi * 16 + h) * 129 * 384
                            for va in range(3):
                                dma("sp", Bx[:, va, bi, :], AP(U_t, base + 127, [[383, 128], [1, 256]]), writes=[b_r])
                        op("pool", lambda e: e.tensor_scalar(Bx[:, 1, :, 192:256], Bx[:, 1, :, 192:256], negc, None, ALU.add),
                           reads=[b_r, cst_r], writes=[b_r])
                        op("pool", lambda e: e.tensor_scalar(Bx[:, 2, :, 0:64], Bx[:, 2, :, 0:64], negc, None, ALU.add),
                           reads=[b_r, cst_r], writes=[b_r])
                        op("dve", lambda e: e.memset(acc[:], 0.0), reads=acc_r, writes=acc_r)
                        units = []
                        for bi, dd in ((0, 1), (1, 4), (2, 16)):
                            mpg = 32 // dd
                            for g in range(dd):
                                for m in range(mpg):
                                    units.append((bi, dd, g, m, mpg))

                        def rcols(dd, g, l0, n):
                            s0 = g + dd * l0
                            return slice(s0, s0 + dd * (n - 1) + 1, dd)

                        def qrange(m, mpg):
                            lo = 64 if m == 0 else 0
                            hi = 192 if m == mpg - 1 else 256
                            return lo, hi

                        def stage1(u, ui):
                            bi, dd, g, m, mpg = u
                            i3 = ui % NB
                            bank = i3
                            lo, hi = qrange(m, mpg)
                            qsl = rcols(dd, g, 128 * m - 64 + lo, hi - lo)
                            op("pe", lambda e: e.matmul(ps[:, bank, lo:hi], KT[:, rcols(dd, g, 128 * m, 128)], QT[:, qsl],
                                                        start=True, stop=True),
                               reads=[kk_r, q_r], writes=[psr[bank]])
                            mid = mpg // 2
                            va = 1 if m == mid - 1 else (2 if m == mid else 0)
                            op("dve", lambda e: e.tensor_tensor(scs[i3][:, lo:hi], ps[:, bank, lo:hi], Bx[:, va, bi, lo:hi], ALU.add),
                               reads=[psr[bank], b_r], writes=[scs_r[i3]])
                            op("act", lambda e: e.activation(Pm[i3][:, lo:hi], scs[i3][:, lo:hi], AF.Exp),
                               reads=[scs_r[i3]], writes=[Pm_r[i3]])

                        def stage2(u, ui):
                            bi, dd, g, m, mpg = u
                            i3 = ui % NB
                            bank = 3 + ui % 2
                            lo, hi = qrange(m, mpg)
                            t0 = g * mpg + m
                            op("pe", lambda e: e.matmul(ps[:, bank, lo:hi], Vd[bi][:, t0, :], Pm[i3][:, lo:hi], start=True, stop=True),
                               reads=[v_r, Pm_r[i3]], writes=[psr[bank]], inc=False)
                            op("pe", lambda e: e.matmul(ps[:, bank, 256 + lo:256 + hi], ones_bf[:], Pm[i3][:, lo:hi], start=True, stop=True),
                               reads=[k_r, Pm_r[i3]], writes=[psr[bank]])
                            l0 = 128 * m - 64 + lo
                            cols = rcols(dd, g, l0, hi - lo)
                            c_lo = g + dd * l0
                            c_hi = g + dd * (l0 + hi - lo - 1)
                            ar = [acc_r[b_] for b_ in range(c_lo // 128, c_hi // 128 + 1)]
                            pv = ps[:, bank, :].rearrange("p (a b) -> p a b", a=2)[:, :, lo:hi]
                            op("dve", lambda e: e.tensor_tensor(acc[:, :, cols], pv, acc[:, :, cols], ALU.add),
                               reads=[psr[bank]] + ar, writes=ar)

                        nu = len(units)
                        for ui in range(nu + 2):
                            if ui < nu:
                                stage1(units[ui], ui)
                            if 0 <= ui - 2 < nu:
                                stage2(units[ui - 2], ui - 2)
                        op("act", lambda e: e.activation(acc[:, 1, :], acc[:, 1, :], AF.Ln), reads=acc_r, writes=acc_r)
                        op("act", lambda e: e.activation(acc[:, 1, :], acc[:, 1, :], AF.Exp, scale=-1.0), reads=acc_r, writes=acc_r)
                        op("dve", lambda e: e.tensor_tensor(yo[:], acc[:, 0, :], acc[:, 1, :], ALU.mult), reads=acc_r, writes=[yo_r])
                        dma("sp", YT[16 + h, :, :], yo[:], reads=[yo_r])
                tk.barrier()

            if want("C"):
                with ExitStack() as es:
                    big = sb("cbig", [128, KC, T], F32, es)
                    big_r = [Res() for _ in range(KC)]
                    aT = sb("caT", [128, KC, T], BF16, es)
                    aT_r = Res()
                    wb = [sb(f"cwb{i}", [128, 8192], BF16, es) for i in range(4)]
                    wb_r = [[Res(), Res()] for _ in range(4)]
                    uT = [sb(f"uT{i}", [128, 2, T], BF16, es) for i in range(2)]
                    uT_r = [Res() for _ in range(2)]
                    ur = [sb(f"ur{i}", [128, T], F32, es) for i in range(2)]
                    ur_r = [Res() for _ in range(2)]
                    xs = [sb(f"cxs{i}", [128, T], F32, es) for i in range(3)]
                    xs_r = [Res() for _ in range(3)]
                    sq = [sb(f"csq{i}", [128, T], BF16, es) for i in range(4)]
                    sq_r = [Res() for _ in range(4)]
                    rstd = sb("crstd", [128, T], F32, es)
                    rstd_r = Res()
                    tm = [sb(f"ctm{i}", [128, T], F32, es) for i in range(2)]
                    tm_r = [Res() for _ in range(2)]
                    if last:
                        orow = [sb(f"orow{i}", [128, 1024], F32, es) for i in range(2)]
                        orow_r = [Res() for _ in range(2)]
                    x1_r = [Res() for _ in range(KC)]
                    xt_r = [Res() for _ in range(KC)]
                    NOG = D // 256
                    NFG = DFF // 256
                    stream = [("o", i) for i in range(NOG)]
                    for fg in range(NFG):
                        stream.append(("u", fg))
                        stream.append(("d", fg))
                    SL = len(stream)
                    total = NT * SL
                    st = dict(n=0)

                    def issue_w(idx):
                        kind, i = stream[idx % SL]
                        b = idx % 4
                        if kind == "o":
                            dst = wb[b][:].rearrange("p (k n) -> p k n", k=KC)
                            for hh in range(2):
                                dma("pool", dst[:, hh * 16:(hh + 1) * 16, :],
                                    w_out[l, hh * 2048:(hh + 1) * 2048, i * 256:(i + 1) * 256].rearrange("(k p) n -> p k n", p=128),
                                    writes=[wb_r[b][hh]])
                        elif kind == "u":
                            dst = wb[b][:].rearrange("p (k n) -> p k n", k=KC)
                            for hh in range(2):
                                dma("pool", dst[:, hh * 16:(hh + 1) * 16, :],
                                    w_up[l, hh * 2048:(hh + 1) * 2048, i * 256:(i + 1) * 256].rearrange("(k p) n -> p k n", p=128),
                                    writes=[wb_r[b][hh]])
                        else:
                            dst = wb[b][:].rearrange("p (f n) -> p f n", f=2)
                            for hh in range(2):
                                dma("pool", dst[:, hh, :], w_down[l, i * 256 + hh * 128:i * 256 + (hh + 1) * 128, :],
                                    writes=[wb_r[b][hh]])

                    def prefetch(upto):
                        while st["n"] < min(total, upto):
                            issue_w(st["n"])
                            st["n"] += 1

                    cnt = dict(ps=0, sq=0, xs=0, tm=0, pd=0, u=0, orow=0)

                    pend = []

                    def ss_square(k):
                        b = cnt["sq"] % 4
                        cnt["sq"] += 1
                        op("act", lambda e: e.activation(sq[b][:], big[:, k, :], AF.Square),
                           reads=[big_r[k]], writes=[sq_r[b]])
                        pend.append((k, b))

                    def ss_flush(keep=0):
                        while len(pend) > keep:
                            k, b = pend.pop(0)
                            op("pe", lambda e: e.matmul(ps[:, 6, :], om4096[:], sq[b][:], start=(k == 0), stop=(k == KC - 1)),
                               reads=[sq_r[b], k_r], writes=[psr[6]])

                    def ss_rstd():
                        ss_flush(0)
                        op("act", lambda e: e.activation(rstd[:], ps[:, 6, :], AF.Sqrt, bias=epsc[:, 0:1]),
                           reads=[psr[6], k_r], writes=[rstd_r])
                        op("dve", lambda e: e.reciprocal(rstd[:], rstd[:]), reads=[rstd_r], writes=[rstd_r])

                    for tt in range(NT):
                        tok = slice(tt * T, (tt + 1) * T)
                        base = tt * SL
                        prefetch(base + 3)
                        for q4 in range(4):
                            dma("sp", aT[:, q4 * 8:(q4 + 1) * 8, :],
                                YT[q4 * 8:(q4 + 1) * 8, :, tok].rearrange("k p t -> p k t"), writes=[aT_r])
                        for og in range(NOG):
                            idx = base + og
                            prefetch(idx + 3)
                            b = idx % 4
                            W = wb[b][:].rearrange("p (k n) -> p k n", k=KC)
                            for c2 in range(2):
                                oc = og * 2 + c2
                                bank = cnt["ps"] % 2
                                cnt["ps"] += 1
                                for k in range(KC):
                                    op("pe", lambda e: e.matmul(ps[:, bank, :], W[:, k, c2 * 128:(c2 + 1) * 128], aT[:, k, :],
                                                                start=(k == 0), stop=(k == KC - 1)),
                                       reads=[aT_r, wb_r[b][k // 16]], writes=[psr[bank]], inc=(k == KC - 1))
                                ss_flush(1)
                                op("dve", lambda e: e.tensor_copy(big[:, oc, :], ps[:, bank, :]),
                                   reads=[psr[bank]], writes=[big_r[oc]])
                                ss_square(oc)
                        ss_rstd()
                        def ld4(oc):
                            dma("sp", xs[oc % 3][:], XT[oc, :, tok], reads=[xt_r[oc]], writes=[xs_r[oc % 3]])
                        ld4(0)
                        ld4(1)
                        for oc in range(KC):
                            xb = oc % 3
                            if oc + 2 < KC:
                                ld4(oc + 2)
                            tb = cnt["tm"] % 2
                            cnt["tm"] += 1
                            op("dve", lambda e: e.scalar_tensor_tensor(tm[tb][:], big[:, oc, :], gain(1, l, oc), rstd[:],
                                                                       ALU.mult, ALU.mult),
                               reads=[big_r[oc], gains_r, rstd_r], writes=[tm_r[tb]])
                            op("pool", lambda e: e.tensor_tensor(big[:, oc, :], tm[tb][:], xs[xb][:], ALU.add),
                               reads=[tm_r[tb], xs_r[xb]], writes=[big_r[oc]])
                            dma("sp", X1T[oc, :, tok], big[:, oc, :], reads=[big_r[oc]], writes=[x1_r[oc]])
                            ss_square(oc)
                            ss_flush(2)
                        ss_rstd()
                        for k in range(KC):
                            en = "dve"
                            op(en, lambda e: e.scalar_tensor_tensor(aT[:, k, :], big[:, k, :], gain(2, l, k), rstd[:],
                                                                    ALU.mult, ALU.mult),
                               reads=[big_r[k], rstd_r, gains_r], writes=[aT_r])
                        for fg in range(NFG):
                            iu = base + NOG + 2 * fg
                            prefetch(iu + 4)
                            bu = iu % 4
                            bd = (iu + 1) % 4
                            Wu = wb[bu][:].rearrange("p (k n) -> p k n", k=KC)
                            Wd = wb[bd][:].rearrange("p (f n) -> p f n", f=2)
                            u2 = cnt["u"] % 2
                            cnt["u"] += 1
                            for f in range(2):
                                bank = cnt["ps"] % 2
                                cnt["ps"] += 1
                                for k in range(KC):
                                    op("pe", lambda e: e.matmul(ps[:, bank, :], Wu[:, k, f * 128:(f + 1) * 128], aT[:, k, :],
                                                                start=(k == 0), stop=(k == KC - 1)),
                                       reads=[aT_r, wb_r[bu][k // 16]], writes=[psr[bank]], inc=(k == KC - 1))
                                op("act", lambda e: e.activation(ur[f][:], ps[:, bank, :], AF.Relu),
                                   reads=[psr[bank]], writes=[ur_r[f]])
                                op("act", lambda e: e.activation(uT[u2][:, f, :], ur[f][:], AF.Square),
                                   reads=[ur_r[f]], writes=[uT_r[u2]])
                            for oc in range(KC):
                                bank = 2 + cnt["pd"] % 4
                                cnt["pd"] += 1
                                for f in range(2):
                                    op("pe", lambda e: e.matmul(ps[:, bank, :], Wd[:, f, oc * 128:(oc + 1) * 128], uT[u2][:, f, :],
                                                                start=(f == 0), stop=(f == 1)),
                                       reads=[uT_r[u2], wb_r[bd][f]], writes=[psr[bank]], inc=(f == 1))
                                if fg == NFG - 1:
                                    ss_flush(2)
                                if fg == 0:
                                    if oc % 2 == 0:
                                        op("dve", lambda e: e.tensor_copy(big[:, oc, :], ps[:, bank, :]),
                                           reads=[psr[bank]], writes=[big_r[oc]])
                                    else:
                                        op("act", lambda e: e.activation(big[:, oc, :], ps[:, bank, :], AF.Copy),
                                           reads=[psr[bank]], writes=[big_r[oc]])
                                elif oc % 2 == 0:
                                    op("dve", lambda e: e.tensor_tensor(big[:, oc, :], ps[:, bank, :], big[:, oc, :], ALU.add),
                                       reads=[psr[bank], big_r[oc]], writes=[big_r[oc]])
                                else:
                                    tb = cnt["tm"] % 2
                                    cnt["tm"] += 1
                                    op("act", lambda e: e.activation(tm[tb][:], ps[:, bank, :], AF.Copy),
                                       reads=[psr[bank]], writes=[tm_r[tb]])
                                    op("pool", lambda e: e.tensor_tensor(big[:, oc, :], tm[tb][:], big[:, oc, :], ALU.add),
                                       reads=[tm_r[tb], big_r[oc]], writes=[big_r[oc]])
                                if fg == NFG - 1:
                                    ss_square(oc)
                        ss_rstd()
                        def ld7(oc):
                            dma("sp", xs[oc % 3][:], X1T[oc, :, tok], reads=[x1_r[oc]], writes=[xs_r[oc % 3]])
                        ld7(0)
                        ld7(1)
                        for oc in range(KC):
                            xb = oc % 3
                            if oc + 2 < KC:
                                ld7(oc + 2)
                            tb = cnt["tm"] % 2
                            cnt["tm"] += 1
                            op("dve", lambda e: e.scalar_tensor_tensor(tm[tb][:], big[:, oc, :], gain(3, l, oc), rstd[:],
                                                                       ALU.mult, ALU.mult),
                               reads=[big_r[oc], gains_r, rstd_r], writes=[tm_r[tb]])
                            op("pool", lambda e: e.tensor_tensor(big[:, oc, :], tm[tb][:], xs[xb][:], ALU.add),
                               reads=[tm_r[tb], xs_r[xb]], writes=[big_r[oc]])
                            if not last:
                                dma("sp", XT[oc, :, tok], big[:, oc, :], reads=[big_r[oc]], writes=[xt_r[oc]])
                        if last:
                            for j in range(T // 128):
                                for hf in range(4):
                                    ob = cnt["orow"] % 2
                                    cnt["orow"] += 1
                                    for k4 in range(2):
                                        for kk in range(4):
                                            oc = hf * 8 + k4 * 4 + kk
                                            op("pe", lambda e: e.transpose(ps[:, 7, kk * 128:(kk + 1) * 128],
                                                                           big[:, oc, j * 128:(j + 1) * 128], idn32),
                                               reads=[big_r[oc], cst_r], writes=[psr[7]], inc=(kk == 3))
                                        if k4 % 2 == 0:
                                            op("act", lambda e: e.activation(orow[ob][:, k4 * 512:(k4 + 1) * 512], ps[:, 7, :], AF.Copy),
                                               reads=[psr[7]], writes=[orow_r[ob]])
                                        else:
                                            op("dve", lambda e: e.tensor_copy(orow[ob][:, k4 * 512:(k4 + 1) * 512], ps[:, 7, :]),
                                               reads=[psr[7]], writes=[orow_r[ob]])
                                    r0 = tt * T + j * 128
                                    dma("sp", y_out[r0:r0 + 128, hf * 1024:(hf + 1) * 1024], orow[ob][:], reads=[orow_r[ob]])
                tk.barrier()
        tk.barrier()
    return nc


def _host_consts(pos, boundary):
    half = 64
    inv = 10000.0 ** (-np.arange(half, dtype=np.float64) / half)
    ang = pos.astype(np.float64)[None, :] * inv[:, None]
    cos = np.concatenate([np.cos(ang), np.cos(ang)], axis=0)
    sin = np.concatenate([-np.sin(ang), np.sin(ang)], axis=0)
    ksc = 128.0 ** -0.5
    rot = np.stack([cos, sin, cos * ksc, sin * ksc]).astype(np.float32)
    cst = np.zeros((128, CO_N), np.float32)
    j = np.arange(128)[:, None].astype(np.float64)
    i = np.arange(128)[None, :].astype(np.float64)
    cst[:, CO_DPOS:CO_DPOS + 128] = np.maximum(i - j, 0)
    cst[:, CO_DNEG:CO_DNEG + 128] = np.maximum(j - i, 0)
    cst[:, CO_IP1:CO_IP1 + 128] = np.broadcast_to(i + 1, (128, 128))
    cst[:, CO_CMI:CO_CMI + 128] = np.broadcast_to(128 - i, (128, 128))
    cst[:, CO_IDN:CO_IDN + 128] = np.eye(128)
    sw = np.zeros((128, 128))
    for m in range(128):
        sw[(m + 64) % 128, m] = 1.0
    cst[:, CO_SWP:CO_SWP + 128] = sw
    cst[:, CO_CM1J] = 127 - np.arange(128)
    cst[:, CO_JCOL] = np.arange(128)
    cst[:, CO_C128] = 128.0
    cst[:, CO_NEGC] = NEG if boundary else 0.0
    keepf = np.ones(32)
    keepb = np.ones(32)
    if boundary:
        keepf[16] = 0.0
        keepb[15] = 0.0
    cst[:, CO_KEEPF:CO_KEEPF + 32] = keepf[None, :]
    cst[:, CO_KEEPB:CO_KEEPB + 32] = keepb[None, :]
    return rot, cst


def _rel_bucket_np(rel):
    nbk = 16
    max_exact = 8
    base = np.where(rel > 0, nbk, 0)
    n = np.abs(rel)
    nf = np.maximum(n, 1).astype(np.float32)
    large = max_exact + (np.log(nf / np.float32(max_exact)) / np.float32(np.log(1024 / max_exact))
                         * np.float32(nbk - max_exact)).astype(np.int32)
    large = np.minimum(large, nbk - 1)
    return base + np.where(n < max_exact, n, large)


def _host_onehot():
    oh = np.zeros((3, 33, 384), np.float32)
    for bi, dd in enumerate((1, 4, 16)):
        for xx in range(384):
            x = xx - 64
            delta = 127 - x
            if abs(delta) > 64:
                oh[bi, 32, xx] = 1.0
            else:
                b = int(_rel_bucket_np(np.array([delta * dd], dtype=np.int32))[0])
                oh[bi, b, xx] = 1.0
    return oh


_NC_CACHE = {}


def kernel(x_prompt, x_sample, rel_bias_table, w_in, ret_log_decay, ret_norm_gain, w_out, w_up, w_down,
           norm_mix_pre, norm_mix_post, norm_mlp_pre, norm_mlp_post, _dbg=None):
    f = lambda a: np.ascontiguousarray(np.asarray(a, dtype=np.float32))
    x_prompt, x_sample = f(x_prompt), f(x_sample)
    if _dbg and _dbg.get("smallw"):
        z = np.zeros((1, 1, 1), np.float32)
        w_in = z if "w_in" in _dbg["smallw"] else w_in
        w_out = z if "w_out" in _dbg["smallw"] else w_out
        w_up = z if "w_up" in _dbg["smallw"] else w_up
        w_down = z if "w_down" in _dbg["smallw"] else w_down
    shared = dict(w_in=f(w_in), w_out=f(w_out), w_up=f(w_up), w_down=f(w_down),
                  rel_bias_table=f(rel_bias_table), ret_log_decay=f(ret_log_decay),
                  ret_norm_gain=f(ret_norm_gain), norm_mix_pre=f(norm_mix_pre), norm_mix_post=f(norm_mix_post),
                  norm_mlp_pre=f(norm_mlp_pre), norm_mlp_post=f(norm_mlp_post), onehot=_host_onehot())
    pos_p = np.concatenate([np.arange(2048), np.arange(2048)])
    pos_s = np.arange(4096)
    rot_p, cst_p = _host_consts(pos_p, True)
    rot_s, cst_s = _host_consts(pos_s, False)
    cores = list(range(8)) if not (_dbg and "cores" in _dbg) else _dbg["cores"]
    in_maps = []
    for c in cores:
        m = dict(shared)
        if c < 4:
            m["x"] = x_prompt[2 * c:2 * c + 2].reshape(NTOK, D)
            m["rot"], m["consts"] = rot_p, cst_p
        else:
            m["x"] = x_sample[c - 4].reshape(NTOK, D)
            m["rot"], m["consts"] = rot_s, cst_s
        in_maps.append(m)
    nc = build(_dbg)
    res = run_bass_kernel_spmd(nc, in_maps, core_ids=list(range(len(cores))))
    if _dbg and _dbg.get("raw"):
        return res
    outs = [r["y"] for r in res.results]
    y_prompt = np.stack([o.reshape(2, 2048, D) for o in outs[:4]]).reshape(8, 2048, D)
    y_sample = np.stack(outs[4:8]).reshape(4, 4096, D)
    return (y_prompt.astype(np.float32), y_sample.astype(np.float32))
```
